# Optimizing a Trainium2 kernel written in Bass

```python
import jax, jax.numpy as jnp
from jax import lax
import numpy as np

D_MODEL = 1024
BATCH = 16
SEQ = 2048
DEPTH = 1

HGRN_EXPAND = 128
HGRN_WIDTH = D_MODEL
HGRN_HEADS = HGRN_WIDTH // HGRN_EXPAND
HGRN_HEAD_V = HGRN_WIDTH // HGRN_HEADS
HGRN_CHUNK = 16
POOL_WINDOWS = (2, 4, 8, 16)
POOL_GROUPS = len(POOL_WINDOWS)
POOL_WIDTH = D_MODEL
POOL_GROUP_DIM = POOL_WIDTH // POOL_GROUPS
D_FF = 4 * D_MODEL
IN_COLS = 4 * HGRN_WIDTH + POOL_WIDTH + 2 * D_MODEL
DEEPNORM_ALPHA = (2.0 * DEPTH) ** 0.25
DEEPNORM_BETA = (8.0 * DEPTH) ** -0.25
LN_EPS = 1e-5
RMS_EPS = 1e-6

kernel_name = "hgrn2_multiscale_pool_gated_hybrid_deepnorm"


def layer_norm(x, g, b):
    xf = x.astype(jnp.float32)
    mu = jnp.mean(xf, axis=-1, keepdims=True)
    var = jnp.mean(jnp.square(xf - mu), axis=-1, keepdims=True)
    y = (xf - mu) * lax.rsqrt(var + LN_EPS)
    return (y * g.astype(jnp.float32) + b.astype(jnp.float32)).astype(x.dtype)


def hgrn2_chunked(q, k, v, log_f):
    B, S, H, dk = q.shape
    dv = v.shape[-1]
    n = S // HGRN_CHUNK

    def to_chunks(t):
        return t.reshape(B, n, HGRN_CHUNK, H, t.shape[-1]).transpose(1, 0, 3, 2, 4)

    q, k, v, log_f = to_chunks(q), to_chunks(k), to_chunks(v), to_chunks(log_f)
    G = jnp.cumsum(log_f, axis=3)
    G_last = G[:, :, :, -1:, :]
    q_dec = q * jnp.exp(G)
    k_inv = k * jnp.exp(-G)
    k_to_end = k * jnp.exp(G_last - G)
    causal = jnp.tril(jnp.ones((HGRN_CHUNK, HGRN_CHUNK), dtype=bool))
    scores = jnp.einsum('nbhtd,nbhsd->nbhts', q_dec, k_inv)
    scores = jnp.where(causal, scores, 0.0)
    o_intra = jnp.einsum('nbhts,nbhsv->nbhtv', scores, v)

    def step(state, xs):
        q_c, k_c, v_c, decay_c = xs
        o_inter = jnp.einsum('bhtd,bhdv->bhtv', q_c, state)
        new_state = jnp.swapaxes(decay_c, -1, -2) * state + jnp.einsum('bhsd,bhsv->bhdv', k_c, v_c)
        return new_state, o_inter

    state0 = jnp.zeros((B, H, dk, dv), jnp.float32)
    _, o_inter = lax.scan(step, state0, (q_dec, k_to_end, v, jnp.exp(G_last)))
    o = o_intra + o_inter
    return o.transpose(1, 0, 3, 2, 4).reshape(B, S, H, dv)


def causal_multiscale_pool(v):
    B, S, _ = v.shape
    vg = v.reshape(B, S, POOL_GROUPS, POOL_GROUP_DIM).astype(jnp.float32)
    csum = jnp.cumsum(vg, axis=1)
    pos = jnp.arange(S)
    outs = []
    for g, w in enumerate(POOL_WINDOWS):
        c = csum[:, :, g]
        lagged = jnp.pad(c, ((0, 0), (w, 0), (0, 0)))[:, :S]
        count = jnp.minimum(pos + 1, w).astype(jnp.float32)[:, None]
        outs.append((c - lagged) / count - vg[:, :, g])
    return jnp.stack(outs, axis=2).astype(v.dtype)


def hybrid_mixer(x, w_in, lower_bound, hgrn_norm_g, w_a, w_pool, pool_scale, w_out):
    B, S, _ = x.shape
    proj = x @ w_in
    hw = HGRN_WIDTH
    splits = [hw, 2 * hw, 3 * hw, 4 * hw, 4 * hw + POOL_WIDTH, 4 * hw + POOL_WIDTH + D_MODEL]
    q, f_pre, i_val, o_gate, pool_v, gate_a, gate_b = jnp.split(proj, splits, axis=-1)

    qf = jax.nn.silu(q.astype(jnp.float32)) * (HGRN_EXPAND ** -0.5)
    lb = lower_bound.astype(jnp.float32)
    f = lb + (1.0 - lb) * jax.nn.sigmoid(f_pre.astype(jnp.float32))
    k = 1.0 - f
    log_f = jnp.log(f)
    shp = (B, S, HGRN_HEADS, HGRN_EXPAND)
    o = hgrn2_chunked(qf.reshape(shp), k.reshape(shp),
                      i_val.astype(jnp.float32).reshape(B, S, HGRN_HEADS, HGRN_HEAD_V), log_f.reshape(shp))
    o = o * lax.rsqrt(jnp.mean(jnp.square(o), axis=-1, keepdims=True) + RMS_EPS)
    o = o.reshape(B, S, hw) * hgrn_norm_g.astype(jnp.float32) * jax.nn.sigmoid(o_gate.astype(jnp.float32))
    a = o.astype(x.dtype) @ w_a

    pooled = causal_multiscale_pool(pool_v)
    b = jnp.einsum('bsgc,gcd->bsgd', pooled, w_pool).reshape(B, S, POOL_WIDTH) * pool_scale

    merged = jax.nn.sigmoid(gate_a) * a + jax.nn.sigmoid(gate_b) * b
    return merged @ w_out


def sq_relu_mlp(x, w_up, w_down):
    return jnp.square(jax.nn.relu(x @ w_up)) @ w_down


def setup_inputs(seed: int = 0) -> dict:
    key = jax.random.key(seed)
    ks = jax.random.split(key, 15)
    f32 = jnp.float32
    nrm = lambda k, s: jax.random.normal(k, s, f32)
    return {
        "x": nrm(ks[0], (BATCH, SEQ, D_MODEL)),
        "w_in": nrm(ks[1], (DEPTH, D_MODEL, IN_COLS)) * D_MODEL ** -0.5,
        "lb_logits": nrm(ks[2], (DEPTH + 1, HGRN_WIDTH)) * 0.5,
        "hgrn_norm_g": 1.0 + 0.05 * nrm(ks[3], (DEPTH, HGRN_WIDTH)),
        "w_a": nrm(ks[4], (DEPTH, HGRN_WIDTH, D_MODEL)) * HGRN_WIDTH ** -0.5,
        "w_pool": nrm(ks[5], (DEPTH, POOL_GROUPS, POOL_GROUP_DIM, POOL_GROUP_DIM)) * POOL_GROUP_DIM ** -0.5,
        "pool_scale": 1.0 + 0.05 * nrm(ks[6], (DEPTH, POOL_WIDTH)),
        "w_out": nrm(ks[7], (DEPTH, D_MODEL, D_MODEL)) * (D_MODEL ** -0.5) * DEEPNORM_BETA,
        "ln1_g": 1.0 + 0.05 * nrm(ks[8], (DEPTH, D_MODEL)),
        "ln1_b": 0.02 * nrm(ks[9], (DEPTH, D_MODEL)),
        "w_up": nrm(ks[10], (DEPTH, D_MODEL, D_FF)) * D_MODEL ** -0.5,
        "w_down": nrm(ks[11], (DEPTH, D_FF, D_MODEL)) * (D_FF ** -0.5) * DEEPNORM_BETA,
        "ln2_g": 1.0 + 0.05 * nrm(ks[12], (DEPTH, D_MODEL)),
        "ln2_b": 0.02 * nrm(ks[13], (DEPTH, D_MODEL)),
    }


def reference(x, w_in, lb_logits, hgrn_norm_g, w_a, w_pool, pool_scale, w_out,
              ln1_g, ln1_b, w_up, w_down, ln2_g, ln2_b):
    lower_bounds = jnp.cumsum(jax.nn.softmax(lb_logits.astype(jnp.float32), axis=0), axis=0)
    for l in range(DEPTH):
        mix = hybrid_mixer(x, w_in[l], lower_bounds[l], hgrn_norm_g[l], w_a[l], w_pool[l],
                           pool_scale[l], w_out[l])
        x = layer_norm(DEEPNORM_ALPHA * x + mix, ln1_g[l], ln1_b[l])
        x = layer_norm(DEEPNORM_ALPHA * x + sq_relu_mlp(x, w_up[l], w_down[l]), ln2_g[l], ln2_b[l])
    return x
```

```python
from contextlib import ExitStack

import numpy as np
import concourse.bass as bass
import concourse.mybir as mybir
from concourse.bass_utils import run_bass_kernel_spmd

F32 = mybir.dt.float32
BF16 = mybir.dt.bfloat16
AF = mybir.ActivationFunctionType
ALU = mybir.AluOpType

N_CORES = 8
D = 1024
SEQ = 2048
TOK_CORE = 4096
T = 256
NT_FULL = TOK_CORE // T
TILES_PER_SEQ = SEQ // T
NPIECE = 68
NSLOT = 6
ALPHA = 2.0 ** 0.25
LN_EPS = 1e-5
RMS_EPS = 1e-6
QSCALE = 128.0 ** -0.5
POOL_W = (2, 4, 8, 16)

PB_WIN = 0
PB_WA = 28
PB_WOUT = 32
PB_WUP = 36
PB_WDOWN = 52


class Reg:
    __slots__ = ("name", "writers", "readers")

    def __init__(self, name):
        self.name = name
        self.writers = []
        self.readers = []


class Sem:
    __slots__ = ("sem", "count")

    def __init__(self, sem):
        self.sem = sem
        self.count = 0


class Op:
    __slots__ = ("eng", "fn", "deps", "raw", "ticket", "signal", "dma")

    def __init__(self, eng, fn, dma):
        self.eng = eng
        self.fn = fn
        self.deps = set()
        self.raw = set()
        self.ticket = None
        self.signal = False
        self.dma = dma


class Sched:
    ENGS = ("pe", "act", "dve", "pool", "sp")

    def __init__(self, nc, stack):
        self.nc = nc
        self.stack = stack
        self.ops = []
        self.esem = {e: Sem(stack.enter_context(nc.semaphore("prog_" + e))) for e in self.ENGS}

    def dma_sem(self, name):
        self.nsem = getattr(self, "nsem", 0) + 1
        return Sem(self.stack.enter_context(self.nc.semaphore(f"{name}_{self.nsem}")))

    def op(self, eng, fn, reads=(), writes=(), dma=None):
        o = Op(eng, fn, dma)
        for r in reads:
            o.deps.update(r.writers)
            o.raw.update(r.writers)
        for w in writes:
            o.deps.update(w.readers)
            o.deps.update(w.writers)
        for r in reads:
            r.readers.append(o)
        for w in writes:
            if w.readers:
                w.writers = [o]
                w.readers = []
            else:
                w.writers.append(o)
        o.deps.discard(o)
        self.ops.append(o)
        return o

    @staticmethod
    def _inorder(d, o):
        return d.dma is None and o.dma is None and d.eng == o.eng and (d.eng == "pe" or d not in o.raw)

    def emit(self):
        nc = self.nc
        ops = self.ops
        for o in ops:
            for d in o.deps:
                if not self._inorder(d, o):
                    d.signal = True
        for o in ops:
            if o.dma is not None:
                o.dma.count += 16
                o.ticket = (o.dma, o.dma.count)
            elif o.signal:
                s = self.esem[o.eng]
                s.count += 1
                o.ticket = (s, s.count)
        per = {e: [o for o in ops if o.eng == e] for e in self.ENGS}
        with nc.Block() as block:
            def body(ename, eobj):
                seen = {}
                for o in per[ename]:
                    need = {}
                    for d in o.deps:
                        if self._inorder(d, o):
                            continue
                        s, v = d.ticket
                        if seen.get(id(s), 0) >= v:
                            continue
                        if need.get(id(s), (None, 0))[1] < v:
                            need[id(s)] = (s, v)
                    for s, v in need.values():
                        eobj.wait_ge(s.sem, v)
                        seen[id(s)] = v
                    ins = o.fn(eobj)
                    if ins is None:
                        assert o.dma is None and not o.signal
                    elif o.dma is not None:
                        ins.then_inc(o.dma.sem, 16)
                    elif o.signal:
                        ins.then_inc(self.esem[ename].sem, 1)

            @block.tensor
            def _(e):
                body("pe", e)

            @block.scalar
            def _(e):
                body("act", e)

            @block.vector
            def _(e):
                body("dve", e)

            @block.gpsimd
            def _(e):
                body("pool", e)

            @block.sync
            def _(e):
                body("sp", e)


def build_nc(ntiles=NT_FULL, dbg=None):
    nc = bass.Bass("TRN2", target_bir_lowering=False)

    def dram(name, shape, dt=F32, kind="ExternalInput"):
        return nc.dram_tensor(name, shape, dt, kind=kind).ap()

    x_d = dram("x", [TOK_CORE, D])
    wp_d = dram("wp", [NPIECE, 128, 2048])
    wpool_d = dram("wpool", [128, 2048])
    cols_d = dram("cols", [128, 32])
    rows_d = dram("rows", [4, D])
    cid_d = dram("c_ident", [128, 128])
    cmask_d = dram("c_mask", [128, 512])
    creset_d = dram("c_reset", [128, 1024])
    cpt_d = dram("c_pt", [128, 12 * 128])
    out_d = dram("out", [TOK_CORE, D], kind="ExternalOutput")
    wscr_d = dram("wscr", [NPIECE, 128, 2048], BF16, kind="Internal")
    dbg_d = {}
    if dbg:
        for k, shp in dbg.items():
            dbg_d[k] = dram("dbg_" + k, list(shp), kind="ExternalOutput")

    with ExitStack() as st:
        S = Sched(nc, st)

        def sb(name, shape, dt=F32):
            return st.enter_context(nc.sbuf_tensor("s_" + name, shape, dt))

        X = [sb(f"X{b}", [128, 2, D]) for b in range(2)]
        xT = sb("xT", [128, 8, T], BF16)
        x1T = sb("x1T", [128, 8, T], BF16)
        wring = [sb(f"wr{s}", [128, 8, 256], BF16) for s in range(NSLOT)]
        wpool_sb = sb("wpool_sb", [128, 8, 256], BF16)
        ident_f = sb("ident_f", [128, 128])
        ident_b = sb("ident_b", [128, 128], BF16)
        mask_b = sb("mask_b", [128, 4, 128], BF16)
        reset_b = sb("reset_b", [128, 1024], BF16)
        pt_b = sb("pt_b", [128, 12, 128], BF16)
        ones_b = sb("ones_b", [128, 128], BF16)
        cols = sb("cols", [128, 32])
        lbt = sb("lbt", [128, 24])
        lnp = sb("lnp", [128, 4, D])
        qs = [sb(f"qs{g}", [128, 4, T]) for g in range(2)]
        F1 = [sb(f"F1{g}", [128, 4, T]) for g in range(2)]
        F2 = sb("F2", [128, 4, T])
        F3 = sb("F3", [128, 4, T])
        E = sb("E", [128, 4, T])
        q_dec = sb("q_dec", [128, 8, T], BF16)
        k_inv = sb("k_inv", [128, 8, T], BF16)
        k_end = sb("k_end", [128, 8, T], BF16)
        k_endT = sb("k_endT", [128, 2, 2, 4, 128], BF16)
        dec = sb("dec", [128, 2, 16])
        v_tok = sb("v_tok", [128, 2, D], BF16)
        og_s = sb("og_s", [128, 8, T])
        pool_v = sb("pool_v", [128, 3, D], BF16)
        sc = [sb(f"sc{r}", [128, 4, 128], BF16) for r in range(4)]
        S32 = sb("S32", [128, 8, 128])
        S16 = sb("S16", [128, 8, 128], BF16)
        o32 = [sb(f"o32_{r}", [128, 2, T]) for r in range(2)]
        osq = [sb(f"osq_{r}", [128, 2, T], BF16) for r in range(2)]
        lt = [sb(f"lt_{r}", [128, 2, T]) for r in range(2)]
        o_gated = sb("o_gated", [128, 8, T], BF16)
        pooledT = sb("pooledT", [128, 8, T], BF16)
        gA = [sb(f"gA{r}", [128, 2, T]) for r in range(2)]
        gB = [sb(f"gB{r}", [128, 2, T]) for r in range(2)]
        t1 = [sb(f"t1_{r}", [128, 2, T]) for r in range(2)]
        t2 = [sb(f"t2_{r}", [128, 2, T]) for r in range(2)]
        merged = sb("merged", [128, 8, T], BF16)
        hT = sb("hT", [128, 32, T], BF16)
        rtmp = [sb(f"rtmp{r}", [128, 2, T]) for r in range(2)]
        lnst = sb("lnst", [128, 16])
        bst = sb("bst", [128, 2, 2, 6])
        mv = sb("mv", [128, 2, 2, 2])

        banks = [st.enter_context(nc.psum_tensor(f"pb{k}", [128, 512], F32)) for k in range(7)]
        bankT = st.enter_context(nc.psum_tensor("pbT", [128, 1024], BF16))

        R = {}

        def reg(name):
            if name not in R:
                R[name] = Reg(name)
            return R[name]

        bank_r = [reg(f"bank{k}") for k in range(7)]
        bankT_r = reg("bankT")
        ring_state = {"k": 0, "lim": 7}

        def next_bank():
            k = ring_state["k"] % ring_state["lim"]
            ring_state["k"] = k + 1
            return banks[k], bank_r[k]

        def dump(name, src_ap, regs, eng="sp"):
            if name in dbg_d:
                S.op("pool", lambda e: e.dma_start(out=dbg_d[name], in_=src_ap), reads=regs,
                     writes=[reg("outdram")], dma=S.dma_sem("dbgs_" + name))

        def ld(dst, src, r, eng="sp"):
            S.op(eng, lambda e: e.dma_start(out=dst, in_=src), writes=[r], dma=S.dma_sem("ld_" + r.name))

        ld(ident_f[:], cid_d, reg("ident_f"))
        ld(cols[:], cols_d, reg("cols"))
        for r_ in range(4):
            ld(lnp[:, r_, :], rows_d[r_:r_ + 1, :].partition_broadcast(128), reg(f"lnp{r_}"))
        ld(mask_b[:].rearrange("p a b -> p (a b)"), cmask_d, reg("mask_b"), eng="pool")
        ld(reset_b[:], creset_d, reg("reset_b"), eng="pool")
        ld(pt_b[:].rearrange("p a b -> p (a b)"), cpt_d, reg("pt_b"), eng="pool")
        for hf in range(2):
            ld(wpool_sb[:].rearrange("p a b -> p (a b)")[:, hf * 1024:(hf + 1) * 1024],
               wpool_d[:, hf * 1024:(hf + 1) * 1024], reg("wpool_sb"), eng="pool")
        S.op("act", lambda e: e.activation(out=ident_b[:], in_=ident_f[:], func=AF.Copy),
             reads=[reg("ident_f")], writes=[reg("ident_b")])
        S.op("dve", lambda e: e.memset(ones_b[:], 1.0 / 128.0), writes=[reg("ones_b")])
        S.op("dve", lambda e: e.tensor_tensor(out=lbt[:, 16:24], in0=cols[:, 0:8], in1=cols[:, 8:16], op=ALU.subtract),
             reads=[reg("cols")], writes=[reg("lbt_s")])
        S.op("act", lambda e: e.activation(out=lbt[:, 0:8], in_=lbt[:, 16:24], func=AF.Sigmoid),
             reads=[reg("lbt_s")], writes=[reg("lb")])
        S.op("dve", lambda e: e.tensor_scalar(out=lbt[:, 8:16], in0=lbt[:, 0:8], scalar1=-1.0, scalar2=1.0,
                                              op0=ALU.mult, op1=ALU.add),
             reads=[reg("lb")], writes=[reg("oml")])
        gcol = cols[:, 16:24]
        pscol = cols[:, 24:32]

        slot_r = [reg(f"slot{s}") for s in range(NSLOT)]
        slot_ld = [S.dma_sem(f"wld{s}") for s in range(NSLOT)]
        slot_st = [S.dma_sem(f"wst{s}") for s in range(NSLOT)]
        scr_r = [reg(f"scr{p}") for p in range(NPIECE)]
        pstate = {"n": 0}
        pending = []

        def flush_pending(force=False):
            keep = []
            for item in pending:
                item[0] -= 1
                if force or item[0] <= 0:
                    item[1]()
                else:
                    keep.append(item)
            pending[:] = keep

        def get_piece(ti, pidx):
            s = pstate["n"] % NSLOT
            pstate["n"] += 1
            slot = wring[s]
            flat = slot[:].rearrange("p a b -> p (a b)")
            if ti == 0:
                S.op("pool", lambda e: e.dma_start(out=flat.rearrange("p (a b) -> p a b", a=2),
                                                   in_=wp_d[pidx].rearrange("p (a b) -> p a b", a=2)),
                     writes=[slot_r[s]], dma=slot_ld[s])
                if ntiles > 1:
                    S.op("sp", lambda e: e.dma_start(out=wscr_d[pidx], in_=flat),
                         reads=[slot_r[s]], writes=[scr_r[pidx]], dma=slot_st[s])
            else:
                S.op("sp", lambda e: e.dma_start(out=flat, in_=wscr_d[pidx]),
                     reads=[scr_r[pidx]], writes=[slot_r[s]], dma=slot_ld[s])
            flush_pending()
            return slot, slot_r[s]

        x_sem = [S.dma_sem(f"xld{b}") for b in range(2)]
        o_sem = [S.dma_sem(f"ost{b}") for b in range(2)]
        Xr = [[reg(f"X{b}_{j}") for j in range(2)] for b in range(2)]

        def load_x(ti):
            b = ti % 2
            src = x_d[ti * T:(ti + 1) * T, :].rearrange("(j p) f -> p j f", p=128)
            S.op("pool", lambda e: e.dma_start(out=X[b][:], in_=src), writes=Xr[b], dma=x_sem[b])

        def store_out(ti):
            b = ti % 2
            dst = out_d[ti * T:(ti + 1) * T, :].rearrange("(j p) f -> p j f", p=128)
            S.op("pool", lambda e: e.dma_start(out=dst, in_=X[b][:]), reads=Xr[b],
                 writes=[reg("outdram")], dma=o_sem[b])

        def transposes(src, src_r, dst, dst_r, evac_eng):
            for kp in range(4):
                bk, br = next_bank()
                bv = bk[:].rearrange("p (a j t) -> p a j t", a=2, j=2)

                def tr(e, kp=kp, bv=bv):
                    ins = None
                    for a in range(2):
                        kc = 2 * kp + a
                        for j in range(2):
                            ins = e.transpose(out=bv[:, a, j, :], in_=src[:, j, kc * 128:(kc + 1) * 128],
                                              identity=ident_f[:])
                    return ins
                S.op("pe", tr, reads=list(src_r) + [reg("ident_f")], writes=[br])
                dv = dst[:, 2 * kp:2 * kp + 2, :]
                bflat = bk[:].rearrange("p (a t) -> p a t", a=2)
                if evac_eng == "act":
                    S.op("act", lambda e, dv=dv, bflat=bflat: e.activation(out=dv, in_=bflat, func=AF.Copy),
                         reads=[br], writes=[dst_r])
                else:
                    S.op("dve", lambda e, dv=dv, bflat=bflat: e.tensor_copy(out=dv, in_=bflat),
                         reads=[br], writes=[dst_r])

        def mm_feat(slot, slot_reg, act, act_r):
            bk, br = next_bank()
            bv = bk[:].rearrange("p (a t) -> p a t", a=2)

            def f(e):
                ins = None
                for a in range(2):
                    for kc in range(8):
                        ins = e.matmul(bv[:, a, :], slot[:, kc, a * 128:(a + 1) * 128], act[:, kc, :],
                                       start=(kc == 0), stop=(kc == 7))
                return ins
            S.op("pe", f, reads=[slot_reg, act_r], writes=[br])
            return bv, br

        def mm_tok(slot, slot_reg, act, act_r, bank=None):
            bk, br = next_bank() if bank is None else (banks[bank], bank_r[bank])
            bv = bk[:].rearrange("p (j c) -> p j c", j=2)

            def f(e):
                ins = None
                for j in range(2):
                    for kc in range(8):
                        ins = e.matmul(bv[:, j, :], act[:, kc, j * 128:(j + 1) * 128], slot[:, kc, :],
                                       start=(kc == 0), stop=(kc == 7))
                return ins
            S.op("pe", f, reads=[slot_reg, act_r], writes=[br])
            return bv, br

        def layer_norm(b, grow, brow, tag):
            Xb = X[b]
            st_r = reg("lnst")
            for j in range(2):
                for hf in range(2):
                    br_ = reg(f"bst{j}{hf}")
                    S.op("dve", lambda e, j=j, hf=hf: e.bn_stats(out=bst[:, j, hf, :],
                                                                 in_=Xb[:, j, hf * 512:(hf + 1) * 512]),
                         reads=[Xr[b][j]], writes=[br_])
                    S.op("dve", lambda e, j=j, hf=hf: e.bn_aggr(out=mv[:, j, hf, :], in_=bst[:, j, hf, :]),
                         reads=[br_], writes=[reg("mv")])
            mA, mB = mv[:, :, 0, 0], mv[:, :, 1, 0]
            vA, vB = mv[:, :, 0, 1], mv[:, :, 1, 1]
            rmv = reg("mv")
            S.op("dve", lambda e: e.tensor_tensor(out=lnst[:, 0:2], in0=mA, in1=mB, op=ALU.add),
                 reads=[rmv], writes=[reg("ln_sm")])
            S.op("dve", lambda e: e.tensor_tensor(out=lnst[:, 2:4], in0=mA, in1=mB, op=ALU.subtract),
                 reads=[rmv], writes=[reg("ln_dm")])
            S.op("dve", lambda e: e.tensor_tensor(out=lnst[:, 4:6], in0=vA, in1=vB, op=ALU.add),
                 reads=[rmv], writes=[reg("ln_sv")])
            S.op("dve", lambda e: e.tensor_tensor(out=lnst[:, 6:8], in0=lnst[:, 2:4], in1=lnst[:, 2:4], op=ALU.mult),
                 reads=[reg("ln_dm")], writes=[reg("ln_d2")])
            S.op("dve", lambda e: e.scalar_tensor_tensor(out=lnst[:, 8:10], in0=lnst[:, 6:8], scalar=0.5,
                                                         in1=lnst[:, 4:6], op0=ALU.mult, op1=ALU.add),
                 reads=[reg("ln_d2"), reg("ln_sv")], writes=[st_r])
            S.op("act", lambda e: e.activation(out=lnst[:, 10:12], in_=lnst[:, 8:10], func=AF.Ln, bias=LN_EPS,
                                               scale=0.5),
                 reads=[st_r], writes=[st_r])
            S.op("act", lambda e: e.activation(out=lnst[:, 10:12], in_=lnst[:, 10:12], func=AF.Exp, scale=-0.5),
                 reads=[st_r], writes=[st_r])
            S.op("dve", lambda e: e.scalar_tensor_tensor(out=lnst[:, 12:14], in0=lnst[:, 0:2], scalar=-0.5,
                                                         in1=lnst[:, 10:12], op0=ALU.mult, op1=ALU.mult),
                 reads=[st_r, reg("ln_sm")], writes=[st_r])
            for j in range(2):
                xj = Xr[b][j]
                S.op("act", lambda e, j=j: e.activation(out=Xb[:, j, :], in_=Xb[:, j, :], func=AF.Identity,
                                                        scale=lnst[:, 10 + j:11 + j], bias=lnst[:, 12 + j:13 + j]),
                     reads=[xj, st_r], writes=[xj])
                S.op("dve", lambda e, j=j: e.tensor_tensor(out=Xb[:, j, :], in0=Xb[:, j, :], in1=lnp[:, grow, :],
                                                           op=ALU.mult),
                     reads=[xj, reg(f"lnp{grow}")], writes=[xj])
                S.op("dve", lambda e, j=j: e.tensor_tensor(out=Xb[:, j, :], in0=Xb[:, j, :], in1=lnp[:, brow, :],
                                                           op=ALU.add),
                     reads=[xj, reg(f"lnp{brow}")], writes=[xj])

        F1f = [F1[g][:].rearrange("p a t -> p (a t)") for g in range(2)]
        qsf = [qs[g][:].rearrange("p a t -> p (a t)") for g in range(2)]
        F2f = F2[:].rearrange("p a t -> p (a t)")
        F3f = F3[:].rearrange("p a t -> p (a t)")
        Ef = E[:].rearrange("p a t -> p (a t)")
        G3 = F3f.rearrange("p (c t) -> p c t", t=64)
        Gd3 = F2f.rearrange("p (c t) -> p c t", t=64)
        E3 = Ef.rearrange("p (c t) -> p c t", t=64)

        def stageA0(ti):
            b = ti % 2
            ring_state["lim"] = 7
            transposes(X[b], Xr[b], xT, reg("xT"), "act")

        def stageA(ti):
            b = ti % 2
            ring_state["lim"] = 7
            for g in range(2):
                rF1, rqs = reg(f"F1_{g}"), reg(f"qs_{g}")
                for p2 in range(2):
                    slot, sr = get_piece(ti, PB_WIN + 4 + 2 * g + p2)
                    bv, br = mm_feat(slot, sr, xT, reg("xT"))
                    S.op("act", lambda e, bv=bv, p2=p2, g=g: e.activation(out=F1[g][:, 2 * p2:2 * p2 + 2, :],
                                                                          in_=bv, func=AF.Sigmoid),
                         reads=[br], writes=[rF1])
                for p2 in range(2):
                    slot, sr = get_piece(ti, PB_WIN + 0 + 2 * g + p2)
                    bv, br = mm_feat(slot, sr, xT, reg("xT"))
                    S.op("act", lambda e, bv=bv, p2=p2, g=g: e.activation(out=qs[g][:, 2 * p2:2 * p2 + 2, :],
                                                                          in_=bv, func=AF.Silu),
                         reads=[br], writes=[rqs])
            for pv in range(4):
                slot, sr = get_piece(ti, PB_WIN + 8 + pv)
                bv, br = mm_tok(slot, sr, xT, reg("xT"))
                S.op("dve", lambda e, bv=bv, pv=pv: e.tensor_copy(out=v_tok[:, :, pv * 256:(pv + 1) * 256], in_=bv),
                     reads=[br], writes=[reg("v_tok")])
            for po in range(4):
                slot, sr = get_piece(ti, PB_WIN + 12 + po)
                bv, br = mm_feat(slot, sr, xT, reg("xT"))
                S.op("act", lambda e, bv=bv, po=po: e.activation(out=og_s[:, 2 * po:2 * po + 2, :], in_=bv,
                                                                 func=AF.Sigmoid),
                     reads=[br], writes=[reg("og_s")])
            steps = []
            rF2, rF3, rE = reg("F2"), reg("F3"), reg("E")
            for g in range(2):
                h0 = 4 * g
                rF1, rqs = reg(f"F1_{g}"), reg(f"qs_{g}")
                for hh in range(4):
                    h = h0 + hh
                    steps.append(lambda hh=hh, h=h, g=g, rF1=rF1: S.op(
                        "dve", lambda e: e.tensor_scalar(out=F1[g][:, hh, :], in0=F1[g][:, hh, :],
                                                         scalar1=lbt[:, 8 + h:9 + h], scalar2=lbt[:, h:h + 1],
                                                         op0=ALU.mult, op1=ALU.add),
                        reads=[rF1, reg("lb"), reg("oml")], writes=[rF1]))
                steps.append(lambda g=g, rF1=rF1: S.op(
                    "act", lambda e: e.activation(out=F2f, in_=F1f[g], func=AF.Ln), reads=[rF1], writes=[rF2]))
                steps.append(lambda: S.op(
                    "dve", lambda e: e.tensor_tensor_scan(out=F3f, data0=reset_b[:], data1=F2f, initial=0.0,
                                                          op0=ALU.mult, op1=ALU.add),
                    reads=[rF2, reg("reset_b")], writes=[rF3]))
                steps.append(lambda g=g, rF1=rF1: S.op(
                    "dve", lambda e: e.tensor_scalar(out=F1f[g], in0=F1f[g], scalar1=-1.0, scalar2=1.0,
                                                     op0=ALU.mult, op1=ALU.add),
                    reads=[rF1], writes=[rF1]))
                steps.append(lambda: S.op(
                    "dve", lambda e: e.tensor_tensor(out=Gd3, in0=G3[:, :, 63:64].to_broadcast([128, 16, 64]),
                                                     in1=G3, op=ALU.subtract),
                    reads=[rF3], writes=[rF2]))
                steps.append(lambda: S.op(
                    "act", lambda e: e.activation(out=Ef, in_=F3f, func=AF.Exp), reads=[rF3], writes=[rE]))
                steps.append(lambda g=g: S.op(
                    "dve", lambda e: e.tensor_copy(out=dec[:, g, :], in_=E3[:, :, 63]),
                    reads=[rE], writes=[reg(f"dec{g}")]))
                steps.append(lambda g=g, h0=h0, rqs=rqs: S.op(
                    "dve", lambda e: e.scalar_tensor_tensor(
                        out=q_dec[:, h0:h0 + 4, :].rearrange("p a t -> p (a t)"), in0=qsf[g], scalar=QSCALE, in1=Ef,
                        op0=ALU.mult, op1=ALU.mult),
                    reads=[rqs, rE], writes=[reg(f"q_dec{g}")]))
                steps.append(lambda: S.op(
                    "act", lambda e: e.activation(out=Ef, in_=F3f, func=AF.Exp, scale=-1.0),
                    reads=[rF3], writes=[rE]))
                steps.append(lambda g=g, h0=h0, rF1=rF1: S.op(
                    "dve", lambda e: e.tensor_tensor(out=k_inv[:, h0:h0 + 4, :].rearrange("p a t -> p (a t)"),
                                                     in0=F1f[g], in1=Ef, op=ALU.mult),
                    reads=[rF1, rE], writes=[reg(f"k_inv{g}")]))
                steps.append(lambda: S.op(
                    "act", lambda e: e.activation(out=F2f, in_=F2f, func=AF.Exp), reads=[rF2], writes=[rF2]))
                steps.append(lambda g=g, h0=h0, rF1=rF1: S.op(
                    "dve", lambda e: e.tensor_tensor(out=k_end[:, h0:h0 + 4, :].rearrange("p a t -> p (a t)"),
                                                      in0=F1f[g], in1=F2f, op=ALU.mult),
                    reads=[rF1, rF2], writes=[reg(f"k_end{g}")]))
            return steps

        deferred = []

        def stageB(ti):
            b = ti % 2
            first = (ti % TILES_PER_SEQ == 0)
            Xb, xr = X[b], Xr[b]
            if first:
                S.op("dve", lambda e: e.memset(S32[:], 0.0), writes=[reg(f"S32_{h}") for h in range(8)])
                S.op("pool", lambda e: e.memset(S16[:], 0.0), writes=[reg(f"S16g{g}") for g in range(2)])
            if ti == 0:
                dump("xT", xT[:], [reg("xT")])
                dump("q_dec", q_dec[:], [reg("q_dec0"), reg("q_dec1")])
                dump("k_inv", k_inv[:], [reg("k_inv0"), reg("k_inv1")])
                dump("k_end", k_end[:], [reg("k_end0"), reg("k_end1")])
                dump("v_tok", v_tok[:], [reg("v_tok")])

            ring_state["lim"] = 3
            ring_state["k"] = 0
            bT = bankT[:].rearrange("p (j a d) -> p j a d", j=2, a=4)
            for g in range(2):
                def trk(e, g=g):
                    ins = None
                    for j in range(2):
                        for hh in range(4):
                            ins = e.transpose(out=bT[:, j, hh, :], in_=k_end[:, 4 * g + hh, j * 128:(j + 1) * 128],
                                              identity=ident_b[:])
                    return ins
                S.op("pe", trk, reads=[reg(f"k_end{g}"), reg("ident_b")], writes=[bankT_r])
                S.op("act", lambda e, g=g: e.activation(out=k_endT[:, g].rearrange("p j a d -> p (j a d)"),
                                                        in_=bankT[:], func=AF.Copy),
                     reads=[bankT_r], writes=[reg(f"k_endT{g}")])
            ob = [banks[3], banks[4], banks[5], banks[6]]
            obr = [bank_r[3], bank_r[4], bank_r[5], bank_r[6]]
            obv = [bb[:].rearrange("p (a t) -> p a t", a=2) for bb in ob]
            sbk, sbr = banks[2], bank_r[2]
            sv = sbk[:].rearrange("p (a t) -> p a t", a=4)
            for j in range(2):
                for g in range(2):
                    def scf(e, j=j, g=g):
                        ins = None
                        for hh in range(4):
                            h = 4 * g + hh
                            ins = e.matmul(sv[:, hh, :], k_inv[:, h, j * 128:(j + 1) * 128],
                                           q_dec[:, h, j * 128:(j + 1) * 128], start=True, stop=True)
                        return ins
                    S.op("pe", scf, reads=[reg(f"k_inv{g}"), reg(f"q_dec{g}")], writes=[sbr])
                    S.op("dve", lambda e, j=j, g=g: e.tensor_tensor(out=sc[2 * j + g][:], in0=sv, in1=mask_b[:],
                                                                   op=ALU.mult),
                         reads=[sbr, reg("mask_b")], writes=[reg(f"sc{2 * j + g}")])
                for cc in range(2):
                    c = 2 * j + cc
                    p0 = 64 * cc
                    for g in range(2):
                        stv = banks[g][:].rearrange("p (a t) -> p a t", a=4)

                        def stf(e, j=j, p0=p0, stv=stv, g=g):
                            ins = None
                            for hh in range(4):
                                h = 4 * g + hh
                                ins = e.matmul(stv[:, hh, :], k_endT[p0:p0 + 64, g, j, hh, :],
                                               v_tok[p0:p0 + 64, j, h * 128:(h + 1) * 128], start=True, stop=True)
                            return ins
                        S.op("pe", stf, reads=[reg(f"k_endT{g}"), reg("v_tok")], writes=[bank_r[g]])
                    for g in range(2):
                        scs = sc[2 * j + g]

                        def of(e, g=g, j=j, p0=p0, c=c, scs=scs):
                            ins = None
                            for hh in range(4):
                                h = 4 * g + hh
                                dst = obv[h // 2][:, h % 2, c * 64:(c + 1) * 64]
                                e.matmul(dst, v_tok[p0:p0 + 64, j, h * 128:(h + 1) * 128],
                                         scs[p0:p0 + 64, hh, p0:p0 + 64], start=True, stop=False)
                                ins = e.matmul(dst, S16[:, h, :], q_dec[:, h, c * 64:(c + 1) * 64],
                                               start=False, stop=True)
                            return ins
                        S.op("pe", of, reads=[reg("v_tok"), reg(f"sc{2 * j + g}"), reg(f"S16g{g}"), reg(f"q_dec{g}")],
                             writes=[obr[2 * g], obr[2 * g + 1]])
                    for g in range(2):
                        stv = banks[g][:].rearrange("p (a t) -> p a t", a=4)
                        for hh in range(4):
                            h = 4 * g + hh
                            dcol = dec[:, g, hh * 4 + c:hh * 4 + c + 1]
                            S.op("dve", lambda e, h=h, hh=hh, dcol=dcol, stv=stv: e.scalar_tensor_tensor(
                                out=S16[:, h, :], in0=S32[:, h, :], scalar=dcol, in1=stv[:, hh, :],
                                op0=ALU.mult, op1=ALU.add),
                                 reads=[reg(f"S32_{h}"), reg(f"dec{g}"), bank_r[g]], writes=[reg(f"S16g{g}")])
                        for hh in range(4):
                            h = 4 * g + hh
                            dcol = dec[:, g, hh * 4 + c:hh * 4 + c + 1]
                            S.op("dve", lambda e, h=h, hh=hh, dcol=dcol, stv=stv: e.scalar_tensor_tensor(
                                out=S32[:, h, :], in0=S32[:, h, :], scalar=dcol, in1=stv[:, hh, :],
                                op0=ALU.mult, op1=ALU.add),
                                 reads=[reg(f"S32_{h}"), reg(f"dec{g}"), bank_r[g]], writes=[reg(f"S32_{h}")])
            qbs = {}

            def n1(r4):
                r2 = r4 % 2
                bkv, bkr = obv[r4], obr[r4]
                osq_, o32_ = osq[r2], o32[r2]
                S.op("act", lambda e: e.activation(out=osq_[:], in_=bkv, func=AF.Square),
                     reads=[bkr], writes=[reg(f"osq{r2}")])
                S.op("act", lambda e: e.activation(out=o32_[:], in_=bkv, func=AF.Copy),
                     reads=[bkr], writes=[reg(f"o32{r2}")])
                qb, qbr = banks[r4 % 2], bank_r[r4 % 2]
                qbs[r4] = (qb, qbr)
                S.op("pe", lambda e: e.matmul(qb[:], ones_b[:], osq_[:].rearrange("p a t -> p (a t)"),
                                              start=True, stop=True),
                     reads=[reg(f"osq{r2}"), reg("ones_b")], writes=[qbr])

            def n2(r4):
                r2 = r4 % 2
                o32_, lt_ = o32[r2], lt[r2]
                qb, qbr = qbs[r4]
                ltf = lt_[:].rearrange("p a t -> p (a t)")
                S.op("act", lambda e: e.activation(out=ltf, in_=qb[:], func=AF.Ln, bias=RMS_EPS),
                     reads=[qbr], writes=[reg(f"lt{r2}")])
                S.op("act", lambda e: e.activation(out=ltf, in_=ltf, func=AF.Exp, scale=-0.5),
                     reads=[reg(f"lt{r2}")], writes=[reg(f"lt{r2}")])
                S.op("dve", lambda e: e.tensor_tensor(out=o32_[:], in0=o32_[:], in1=lt_[:], op=ALU.mult),
                     reads=[reg(f"o32{r2}"), reg(f"lt{r2}")], writes=[reg(f"o32{r2}")])
                for a in range(2):
                    h = 2 * r4 + a
                    S.op("dve", lambda e, h=h, a=a: e.scalar_tensor_tensor(
                        out=o_gated[:, h, :], in0=o32_[:, a, :], scalar=gcol[:, h:h + 1], in1=og_s[:, h, :],
                        op0=ALU.mult, op1=ALU.mult),
                         reads=[reg(f"o32{r2}"), reg("cols"), reg("og_s")], writes=[reg("o_gated")])

            slots3 = [(2 * ti) % 3, (2 * ti + 1) % 3, (2 * ti + 2) % 3]
            pvr = [reg(f"pool_v{s_}") for s_ in range(3)]

            def pool_branch(g, bank_a=None, bank_b=None):
                slot, sr = get_piece(ti, PB_WIN + 16 + g)
                bv, br = mm_tok(slot, sr, xT, reg("xT"), bank=bank_a)
                for j in range(2):
                    s_ = slots3[1 + j]
                    S.op("dve", lambda e, j=j, s_=s_: e.tensor_copy(
                        out=pool_v[:, s_, g * 256:(g + 1) * 256], in_=bv[:, j, :]),
                         reads=[br], writes=[pvr[s_]])
                bk, bkr = next_bank() if bank_b is None else (banks[bank_b], bank_r[bank_b])
                bkv = bk[:].rearrange("p (a t) -> p a t", a=2)

                def pf(e):
                    ins = None
                    for cc in range(2):
                        k8 = 2 * g + cc
                        for j in range(2):
                            start_tile = first and j == 0
                            ptm = pt_b[:, (8 + g) if start_tile else g, :]
                            dst = bkv[:, cc, j * 128:(j + 1) * 128]
                            ins = e.matmul(dst, pool_v[:, slots3[1 + j], k8 * 128:(k8 + 1) * 128], ptm,
                                           start=True, stop=start_tile)
                            if not start_tile:
                                ins = e.matmul(dst, pool_v[:, slots3[j], k8 * 128:(k8 + 1) * 128],
                                               pt_b[:, 4 + g, :], start=False, stop=True)
                    return ins
                S.op("pe", pf, reads=[pvr[0], pvr[1], pvr[2], reg("pt_b")], writes=[bkr])
                S.op("act", lambda e: e.activation(out=pooledT[:, 2 * g:2 * g + 2, :], in_=bkv, func=AF.Copy),
                     reads=[bkr], writes=[reg("pooledT")])

            n1(0)
            n1(1)
            pool_branch(0, bank_a=2, bank_b=3)
            n2(0)
            n2(1)
            n1(2)
            n1(3)
            n2(2)
            n2(3)
            ring_state["lim"] = 7
            pool_branch(1)
            if ti == 0:
                dump("o_gated", o_gated[:], [reg("o_gated")])
            pool_branch(2)
            pool_branch(3)
            if ti == 0:
                dump("pooledT", pooledT[:], [reg("pooledT")])
            while deferred:
                deferred.pop(0)()

            for np_ in range(4):
                r2 = np_ % 2
                slot, sr = get_piece(ti, PB_WIN + 20 + np_)
                bv, br = mm_feat(slot, sr, xT, reg("xT"))
                S.op("act", lambda e, bv=bv, r2=r2: e.activation(out=gA[r2][:], in_=bv, func=AF.Sigmoid),
                     reads=[br], writes=[reg(f"gA{r2}")])
                slot, sr = get_piece(ti, PB_WIN + 24 + np_)
                bv, br = mm_feat(slot, sr, xT, reg("xT"))
                S.op("act", lambda e, bv=bv, r2=r2: e.activation(out=gB[r2][:], in_=bv, func=AF.Sigmoid),
                     reads=[br], writes=[reg(f"gB{r2}")])
                bk, bkr = next_bank()
                bkv = bk[:].rearrange("p (a t) -> p a t", a=2)

                def bf(e, np_=np_, bkv=bkv):
                    ins = None
                    for nn in range(2):
                        for cc in range(2):
                            ins = e.matmul(bkv[:, nn, :], wpool_sb[:, 2 * np_ + cc, nn * 128:(nn + 1) * 128],
                                           pooledT[:, 2 * np_ + cc, :], start=(cc == 0), stop=(cc == 1))
                    return ins
                S.op("pe", bf, reads=[reg("wpool_sb"), reg("pooledT")], writes=[bkr])
                for nn in range(2):
                    S.op("dve", lambda e, nn=nn, np_=np_, bkv=bkv, r2=r2: e.scalar_tensor_tensor(
                        out=t1[r2][:, nn, :], in0=bkv[:, nn, :], scalar=pscol[:, 2 * np_ + nn:2 * np_ + nn + 1],
                        in1=gB[r2][:, nn, :], op0=ALU.mult, op1=ALU.mult),
                         reads=[bkr, reg("cols"), reg(f"gB{r2}")], writes=[reg(f"t1{r2}")])
                slot, sr = get_piece(ti, PB_WA + np_)
                bv, br = mm_feat(slot, sr, o_gated, reg("o_gated"))
                S.op("dve", lambda e, bv=bv, r2=r2: e.tensor_tensor(out=t2[r2][:], in0=bv, in1=gA[r2][:], op=ALU.mult),
                     reads=[br, reg(f"gA{r2}")], writes=[reg(f"t2{r2}")])
                S.op("dve", lambda e, np_=np_, r2=r2: e.tensor_tensor(out=merged[:, 2 * np_:2 * np_ + 2, :],
                                                                      in0=t2[r2][:], in1=t1[r2][:], op=ALU.add),
                     reads=[reg(f"t2{r2}"), reg(f"t1{r2}")], writes=[reg("merged")])
            if ti == 0:
                dump("merged", merged[:], [reg("merged")])

            if ti + 1 < ntiles:
                flush_pending(force=True)
                load_x(ti + 1)

            for mq in range(4):
                slot, sr = get_piece(ti, PB_WOUT + mq)
                bv, br = mm_tok(slot, sr, merged, reg("merged"))
                S.op("dve", lambda e, bv=bv, mq=mq: e.scalar_tensor_tensor(
                    out=Xb[:, :, mq * 256:(mq + 1) * 256], in0=Xb[:, :, mq * 256:(mq + 1) * 256], scalar=ALPHA,
                    in1=bv, op0=ALU.mult, op1=ALU.add),
                     reads=[br] + xr, writes=xr)
            if ti == 0:
                dump("y1", Xb[:], xr)
            if ti + 1 < ntiles:
                stageA0(ti + 1)
            layer_norm(b, 0, 1, "ln1")
            if ti == 0:
                dump("x1", Xb[:], xr)

        def stageC(ti, steps):
            b = ti % 2
            Xb, xr = X[b], Xr[b]
            ring_state["lim"] = 7
            transposes(Xb, xr, x1T, reg("x1T"), "dve")
            for u in range(16):
                slot, sr = get_piece(ti, PB_WUP + u)
                bv, br = mm_feat(slot, sr, x1T, reg("x1T"))
                r2 = u % 2
                S.op("act", lambda e, bv=bv, r2=r2: e.activation(out=rtmp[r2][:], in_=bv, func=AF.Relu),
                     reads=[br], writes=[reg(f"rtmp{r2}")])
                if u % 2 == 0:
                    S.op("dve", lambda e, u=u, r2=r2: e.tensor_tensor(out=hT[:, 2 * u:2 * u + 2, :], in0=rtmp[r2][:],
                                                                     in1=rtmp[r2][:], op=ALU.mult),
                         reads=[reg(f"rtmp{r2}")], writes=[reg(f"hT{u // 4}")])
                else:
                    S.op("act", lambda e, u=u, r2=r2: e.activation(out=hT[:, 2 * u:2 * u + 2, :], in_=rtmp[r2][:],
                                                                  func=AF.Square),
                         reads=[reg(f"rtmp{r2}")], writes=[reg(f"hT{u // 4}")])
                if steps:
                    steps.pop(0)()
            for mq in range(4):
                bj = [next_bank() for _ in range(2)]
                for dp in range(4):
                    slot, sr = get_piece(ti, PB_WDOWN + mq * 4 + dp)

                    def df(e, dp=dp, slot=slot, bj=bj):
                        ins = None
                        for j in range(2):
                            for fk in range(8):
                                ins = e.matmul(bj[j][0][:, 0:256], hT[:, dp * 8 + fk, j * 128:(j + 1) * 128],
                                               slot[:, fk, :],
                                               start=(dp == 0 and fk == 0), stop=(dp == 3 and fk == 7))
                        return ins
                    S.op("pe", df, reads=[sr, reg(f"hT{dp}")], writes=[bj[0][1], bj[1][1]])
                    if steps:
                        steps.pop(0)()
                for j in range(2):
                    S.op("dve", lambda e, bj=bj, mq=mq, j=j: e.scalar_tensor_tensor(
                        out=Xb[:, j, mq * 256:(mq + 1) * 256], in0=Xb[:, j, mq * 256:(mq + 1) * 256], scalar=ALPHA,
                        in1=bj[j][0][:, 0:256], op0=ALU.mult, op1=ALU.add),
                         reads=[bj[j][1], xr[j]], writes=[xr[j]])
            while steps:
                steps.pop(0)()
            if ti == 0:
                dump("hT", hT[:], [reg(f"hT{k}") for k in range(4)])
                dump("y2", Xb[:], xr)

            def ln2_store(ti=ti, b=b):
                layer_norm(b, 2, 3, "ln2")
                pending.append([3, lambda: store_out(ti)])
            deferred.append(ln2_store)

        load_x(0)
        stageA0(0)
        steps0 = stageA(0)
        while steps0:
            steps0.pop(0)()
        for ti in range(ntiles):
            stageB(ti)
            nsteps = stageA(ti + 1) if ti + 1 < ntiles else []
            stageC(ti, nsteps)
        while deferred:
            deferred.pop(0)()
        flush_pending(force=True)
        S.op("sp", lambda e: None, reads=[reg("outdram")])
        S.emit()
    return nc


def _pieces(w, ncolpieces):
    k = w.shape[0] // 128
    return np.ascontiguousarray(w.reshape(k, 128, ncolpieces, 256).transpose(2, 1, 0, 3)).reshape(ncolpieces, 128, k * 256)


def _constants():
    ident = np.eye(128, dtype=np.float32)
    s = np.arange(128)[:, None]
    t = np.arange(128)[None, :]
    m = ((t >= s) & ((t // 64) == (s // 64))).astype(np.float32)
    mask = np.tile(m, (1, 4))
    reset = np.ones((128, 1024), np.float32)
    reset[:, ::64] = 0.0
    pt = np.zeros((128, 12, 128), np.float32)
    for g, w in enumerate(POOL_W):
        d = t - s
        band = ((d >= 0) & (d < w)).astype(np.float32)
        pt[:, g, :] = band / w - (d == 0)
        dprev = t + 128 - s
        pt[:, 4 + g, :] = (dprev < w).astype(np.float32) / w
        cnt = np.minimum(t + 1, w).astype(np.float32)
        pt[:, 8 + g, :] = band / cnt - (d == 0)
    return ident, mask, reset, pt.reshape(128, 12 * 128)


def _host_inputs(x, w_in, lb_logits, hgrn_norm_g, w_a, w_pool, pool_scale, w_out,
                 ln1_g, ln1_b, w_up, w_down, ln2_g, ln2_b):
    f = np.float32
    pieces = [
        _pieces(np.asarray(w_in[0], f), 28),
        _pieces(np.asarray(w_a[0], f), 4),
        _pieces(np.asarray(w_out[0], f), 4),
        _pieces(np.asarray(w_up[0], f), 16),
    ]
    wd = np.asarray(w_down[0], f).reshape(4, 8, 128, 4, 256)
    pieces.append(np.ascontiguousarray(wd.transpose(3, 0, 2, 1, 4)).reshape(16, 128, 2048))
    wp = np.concatenate(pieces, axis=0)
    assert wp.shape == (NPIECE, 128, 2048)
    wpool = np.ascontiguousarray(np.asarray(w_pool[0], f).reshape(4, 2, 128, 256).transpose(2, 0, 1, 3)).reshape(128, 2048)

    def col(v):
        return np.asarray(v, f).reshape(8, 128).T

    cols = np.ascontiguousarray(np.concatenate(
        [col(lb_logits[0]), col(lb_logits[1]), col(hgrn_norm_g[0]), col(pool_scale[0])], axis=1))
    rows = np.ascontiguousarray(np.stack([np.asarray(v[0], f) for v in (ln1_g, ln1_b, ln2_g, ln2_b)]))
    ident, mask, reset, pt = _constants()
    shared = {"wp": wp, "wpool": wpool, "cols": cols, "rows": rows, "c_ident": ident, "c_mask": mask,
              "c_reset": reset, "c_pt": pt}
    xs = np.asarray(x, f).reshape(N_CORES, TOK_CORE, D)
    return [dict(shared, x=np.ascontiguousarray(xs[c])) for c in range(N_CORES)]


def kernel(x, w_in, lb_logits, hgrn_norm_g, w_a, w_pool, pool_scale, w_out,
           ln1_g, ln1_b, w_up, w_down, ln2_g, ln2_b):
    in_maps = _host_inputs(x, w_in, lb_logits, hgrn_norm_g, w_a, w_pool, pool_scale, w_out,
                           ln1_g, ln1_b, w_up, w_down, ln2_g, ln2_b)
    nc = build_nc()
    res = run_bass_kernel_spmd(nc, in_maps, core_ids=list(range(N_CORES)))
    out = np.stack([np.asarray(r["out"], np.float32) for r in res.results], axis=0)
    return out.reshape(16, SEQ, D)
```

```python
from contextlib import ExitStack

import numpy as np
import concourse.bass as bass
import concourse.mybir as mybir
from concourse.bass_utils import run_bass_kernel_spmd

F32 = mybir.dt.float32
BF16 = mybir.dt.bfloat16
AF = mybir.ActivationFunctionType
ALU = mybir.AluOpType

N_CORES = 8
D = 1024
SEQ = 2048
TOK_CORE = 4096
T = 256
NT_FULL = TOK_CORE // T
TILES_PER_SEQ = SEQ // T
NPIECE = 68
NSLOT = 6
ALPHA = 2.0 ** 0.25
LN_EPS = 1e-5
RMS_EPS = 1e-6
QSCALE = 128.0 ** -0.5
POOL_W = (2, 4, 8, 16)

PB_WIN = 0
PB_WA = 28
PB_WOUT = 32
PB_WUP = 36
PB_WDOWN = 52


class Reg:
    __slots__ = ("name", "writers", "readers")

    def __init__(self, name):
        self.name = name
        self.writers = []
        self.readers = []


class Sem:
    __slots__ = ("sem", "count")

    def __init__(self, sem):
        self.sem = sem
        self.count = 0


class Op:
    __slots__ = ("eng", "fn", "deps", "raw", "ticket", "signal", "dma")

    def __init__(self, eng, fn, dma):
        self.eng = eng
        self.fn = fn
        self.deps = set()
        self.raw = set()
        self.ticket = None
        self.signal = False
        self.dma = dma


class Sched:
    ENGS = ("pe", "act", "dve", "pool", "sp")

    def __init__(self, nc, stack):
        self.nc = nc
        self.stack = stack
        self.ops = []
        self.esem = {e: Sem(stack.enter_context(nc.semaphore("prog_" + e))) for e in self.ENGS}

    def dma_sem(self, name):
        self.nsem = getattr(self, "nsem", 0) + 1
        return Sem(self.stack.enter_context(self.nc.semaphore(f"{name}_{self.nsem}")))

    def op(self, eng, fn, reads=(), writes=(), dma=None):
        o = Op(eng, fn, dma)
        for r in reads:
            o.deps.update(r.writers)
            o.raw.update(r.writers)
        for w in writes:
            o.deps.update(w.readers)
            o.deps.update(w.writers)
        for r in reads:
            r.readers.append(o)
        for w in writes:
            if w.readers:
                w.writers = [o]
                w.readers = []
            else:
                w.writers.append(o)
        o.deps.discard(o)
        self.ops.append(o)
        return o

    @staticmethod
    def _inorder(d, o):
        return d.dma is None and o.dma is None and d.eng == o.eng and (d.eng == "pe" or d not in o.raw)

    def emit(self):
        nc = self.nc
        ops = self.ops
        for o in ops:
            for d in o.deps:
                if not self._inorder(d, o):
                    d.signal = True
        for o in ops:
            if o.dma is not None:
                o.dma.count += 16
                o.ticket = (o.dma, o.dma.count)
            elif o.signal:
                s = self.esem[o.eng]
                s.count += 1
                o.ticket = (s, s.count)
        per = {e: [o for o in ops if o.eng == e] for e in self.ENGS}
        with nc.Block() as block:
            def body(ename, eobj):
                seen = {}
                for o in per[ename]:
                    need = {}
                    for d in o.deps:
                        if self._inorder(d, o):
                            continue
                        s, v = d.ticket
                        if seen.get(id(s), 0) >= v:
                            continue
                        if need.get(id(s), (None, 0))[1] < v:
                            need[id(s)] = (s, v)
                    for s, v in need.values():
                        eobj.wait_ge(s.sem, v)
                        seen[id(s)] = v
                    ins = o.fn(eobj)
                    if ins is None:
                        assert o.dma is None and not o.signal
                    elif o.dma is not None:
                        ins.then_inc(o.dma.sem, 16)
                    elif o.signal:
                        ins.then_inc(self.esem[ename].sem, 1)

            @block.tensor
            def _(e):
                body("pe", e)

            @block.scalar
            def _(e):
                body("act", e)

            @block.vector
            def _(e):
                body("dve", e)

            @block.gpsimd
            def _(e):
                body("pool", e)

            @block.sync
            def _(e):
                body("sp", e)


def build_nc(ntiles=NT_FULL, dbg=None):
    nc = bass.Bass("TRN2", target_bir_lowering=False)

    def dram(name, shape, dt=F32, kind="ExternalInput"):
        return nc.dram_tensor(name, shape, dt, kind=kind).ap()

    x_d = dram("x", [TOK_CORE, D])
    wp_d = dram("wp", [NPIECE, 128, 2048])
    wpool_d = dram("wpool", [128, 2048])
    cols_d = dram("cols", [128, 32])
    rows_d = dram("rows", [4, D])
    cid_d = dram("c_ident", [128, 128])
    cmask_d = dram("c_mask", [128, 512])
    creset_d = dram("c_reset", [128, 1024])
    cpt_d = dram("c_pt", [128, 12 * 128])
    out_d = dram("out", [TOK_CORE, D], kind="ExternalOutput")
    wscr_d = dram("wscr", [NPIECE, 128, 2048], BF16, kind="Internal")
    dbg_d = {}
    if dbg:
        for k, shp in dbg.items():
            dbg_d[k] = dram("dbg_" + k, list(shp), kind="ExternalOutput")

    with ExitStack() as st:
        S = Sched(nc, st)

        def sb(name, shape, dt=F32):
            return st.enter_context(nc.sbuf_tensor("s_" + name, shape, dt))

        X = [sb(f"X{b}", [128, 2, D]) for b in range(2)]
        xT = sb("xT", [128, 8, T], BF16)
        x1T = sb("x1T", [128, 8, T], BF16)
        wring = [sb(f"wr{s}", [128, 8, 256], BF16) for s in range(NSLOT)]
        wpool_sb = sb("wpool_sb", [128, 8, 256], BF16)
        ident_f = sb("ident_f", [128, 128])
        ident_b = sb("ident_b", [128, 128], BF16)
        mask_b = sb("mask_b", [128, 4, 128], BF16)
        reset_b = sb("reset_b", [128, 1024], BF16)
        pt_b = sb("pt_b", [128, 12, 128], BF16)
        ones_b = sb("ones_b", [128, 128], BF16)
        cols = sb("cols", [128, 32])
        lbt = sb("lbt", [128, 24])
        lnp = sb("lnp", [128, 4, D])
        qs = [sb(f"qs{g}", [128, 4, T]) for g in range(2)]
        F1 = [sb(f"F1{g}", [128, 4, T]) for g in range(2)]
        F2 = sb("F2", [128, 4, T])
        F3 = sb("F3", [128, 4, T])
        E = sb("E", [128, 4, T])
        q_dec = sb("q_dec", [128, 8, T], BF16)
        k_inv = sb("k_inv", [128, 8, T], BF16)
        k_end = sb("k_end", [128, 8, T], BF16)
        k_endT = sb("k_endT", [128, 2, 2, 4, 128], BF16)
        dec = sb("dec", [128, 2, 16])
        v_tok = sb("v_tok", [128, 2, D], BF16)
        og_s = sb("og_s", [128, 8, T])
        pool_v = sb("pool_v", [128, 3, D], BF16)
        sc = [sb(f"sc{r}", [128, 4, 128], BF16) for r in range(4)]
        S32 = sb("S32", [128, 8, 128])
        S16 = sb("S16", [128, 8, 128], BF16)
        o32 = [sb(f"o32_{r}", [128, 2, T]) for r in range(2)]
        osq = [sb(f"osq_{r}", [128, 2, T], BF16) for r in range(2)]
        lt = [sb(f"lt_{r}", [128, 2, T]) for r in range(2)]
        o_gated = sb("o_gated", [128, 8, T], BF16)
        pooledT = sb("pooledT", [128, 8, T], BF16)
        gA = [sb(f"gA{r}", [128, 2, T]) for r in range(2)]
        gB = [sb(f"gB{r}", [128, 2, T]) for r in range(2)]
        t1 = [sb(f"t1_{r}", [128, 2, T]) for r in range(2)]
        t2 = [sb(f"t2_{r}", [128, 2, T]) for r in range(2)]
        merged = sb("merged", [128, 8, T], BF16)
        hT = sb("hT", [128, 32, T], BF16)
        rtmp = [sb(f"rtmp{r}", [128, 2, T]) for r in range(2)]
        lnst = sb("lnst", [128, 16])
        bst = sb("bst", [128, 2, 2, 6])
        mv = sb("mv", [128, 2, 2, 2])

        banks = [st.enter_context(nc.psum_tensor(f"pb{k}", [128, 512], F32)) for k in range(7)]
        bankT = st.enter_context(nc.psum_tensor("pbT", [128, 1024], BF16))

        R = {}

        def reg(name):
            if name not in R:
                R[name] = Reg(name)
            return R[name]

        bank_r = [reg(f"bank{k}") for k in range(7)]
        bankT_r = reg("bankT")
        ring_state = {"k": 0, "lim": 7}

        def next_bank():
            k = ring_state["k"] % ring_state["lim"]
            ring_state["k"] = k + 1
            return banks[k], bank_r[k]

        def dump(name, src_ap, regs, eng="sp"):
            if name in dbg_d:
                S.op("pool", lambda e: e.dma_start(out=dbg_d[name], in_=src_ap), reads=regs,
                     writes=[reg("outdram")], dma=S.dma_sem("dbgs_" + name))

        def ld(dst, src, r, eng="sp"):
            S.op(eng, lambda e: e.dma_start(out=dst, in_=src), writes=[r], dma=S.dma_sem("ld_" + r.name))

        ld(ident_f[:], cid_d, reg("ident_f"))
        ld(cols[:], cols_d, reg("cols"))
        for r_ in range(4):
            ld(lnp[:, r_, :], rows_d[r_:r_ + 1, :].partition_broadcast(128), reg(f"lnp{r_}"))
        ld(mask_b[:].rearrange("p a b -> p (a b)"), cmask_d, reg("mask_b"), eng="pool")
        ld(reset_b[:], creset_d, reg("reset_b"), eng="pool")
        ld(pt_b[:].rearrange("p a b -> p (a b)"), cpt_d, reg("pt_b"), eng="pool")
        for hf in range(2):
            ld(wpool_sb[:].rearrange("p a b -> p (a b)")[:, hf * 1024:(hf + 1) * 1024],
               wpool_d[:, hf * 1024:(hf + 1) * 1024], reg("wpool_sb"), eng="pool")
        S.op("act", lambda e: e.activation(out=ident_b[:], in_=ident_f[:], func=AF.Copy),
             reads=[reg("ident_f")], writes=[reg("ident_b")])
        S.op("dve", lambda e: e.memset(ones_b[:], 1.0 / 128.0), writes=[reg("ones_b")])
        S.op("dve", lambda e: e.tensor_tensor(out=lbt[:, 16:24], in0=cols[:, 0:8], in1=cols[:, 8:16], op=ALU.subtract),
             reads=[reg("cols")], writes=[reg("lbt_s")])
        S.op("act", lambda e: e.activation(out=lbt[:, 0:8], in_=lbt[:, 16:24], func=AF.Sigmoid),
             reads=[reg("lbt_s")], writes=[reg("lb")])
        S.op("dve", lambda e: e.tensor_scalar(out=lbt[:, 8:16], in0=lbt[:, 0:8], scalar1=-1.0, scalar2=1.0,
                                              op0=ALU.mult, op1=ALU.add),
             reads=[reg("lb")], writes=[reg("oml")])
        gcol = cols[:, 16:24]
        pscol = cols[:, 24:32]

        slot_r = [reg(f"slot{s}") for s in range(NSLOT)]
        slot_ld = [S.dma_sem(f"wld{s}") for s in range(NSLOT)]
        slot_ld_sw = [S.dma_sem(f"wlds{s}") for s in range(NSLOT)]
        slot_st = [S.dma_sem(f"wst{s}") for s in range(NSLOT)]
        scr_r = [reg(f"scr{p}") for p in range(NPIECE)]
        pstate = {"n": 0}
        pending = []

        def flush_pending(force=False):
            keep = []
            for item in pending:
                item[0] -= 1
                if force or item[0] <= 0:
                    item[1]()
                else:
                    keep.append(item)
            pending[:] = keep

        def get_piece(ti, pidx):
            s = pstate["n"] % NSLOT
            pstate["n"] += 1
            slot = wring[s]
            flat = slot[:].rearrange("p a b -> p (a b)")
            if ti == 0:
                S.op("pool", lambda e: e.dma_start(out=flat.rearrange("p (a b) -> p a b", a=2),
                                                   in_=wp_d[pidx].rearrange("p (a b) -> p a b", a=2)),
                     writes=[slot_r[s]], dma=slot_ld_sw[s])
                if ntiles > 1:
                    S.op("sp", lambda e: e.dma_start(out=wscr_d[pidx], in_=flat),
                         reads=[slot_r[s]], writes=[scr_r[pidx]], dma=slot_st[s])
            else:
                S.op("sp", lambda e: e.dma_start(out=flat, in_=wscr_d[pidx]),
                     reads=[scr_r[pidx]], writes=[slot_r[s]], dma=slot_ld[s])
            flush_pending()
            return slot, slot_r[s]

        x_sem = [S.dma_sem(f"xld{b}") for b in range(2)]
        o_sem = [S.dma_sem(f"ost{b}") for b in range(2)]
        Xr = [[reg(f"X{b}_{j}") for j in range(2)] for b in range(2)]

        def load_x(ti):
            b = ti % 2
            src = x_d[ti * T:(ti + 1) * T, :].rearrange("(j p) f -> p j f", p=128)
            S.op("pool", lambda e: e.dma_start(out=X[b][:], in_=src), writes=Xr[b], dma=x_sem[b])

        def store_out(ti):
            b = ti % 2
            dst = out_d[ti * T:(ti + 1) * T, :].rearrange("(j p) f -> p j f", p=128)
            S.op("pool", lambda e: e.dma_start(out=dst, in_=X[b][:]), reads=Xr[b],
                 writes=[reg("outdram")], dma=o_sem[b])

        def transposes(src, src_r, dst, dst_r, evac_eng):
            for kp in range(4):
                bk, br = next_bank()
                bv = bk[:].rearrange("p (a j t) -> p a j t", a=2, j=2)

                def tr(e, kp=kp, bv=bv):
                    ins = None
                    for a in range(2):
                        kc = 2 * kp + a
                        for j in range(2):
                            ins = e.transpose(out=bv[:, a, j, :], in_=src[:, j, kc * 128:(kc + 1) * 128],
                                              identity=ident_f[:])
                    return ins
                S.op("pe", tr, reads=list(src_r) + [reg("ident_f")], writes=[br])
                dv = dst[:, 2 * kp:2 * kp + 2, :]
                bflat = bk[:].rearrange("p (a t) -> p a t", a=2)
                if evac_eng == "act":
                    S.op("act", lambda e, dv=dv, bflat=bflat: e.activation(out=dv, in_=bflat, func=AF.Copy),
                         reads=[br], writes=[dst_r])
                else:
                    S.op("dve", lambda e, dv=dv, bflat=bflat: e.tensor_copy(out=dv, in_=bflat),
                         reads=[br], writes=[dst_r])

        def mm_feat(slot, slot_reg, act, act_r):
            bk, br = next_bank()
            bv = bk[:].rearrange("p (a t) -> p a t", a=2)

            def f(e):
                ins = None
                for a in range(2):
                    for kc in range(8):
                        ins = e.matmul(bv[:, a, :], slot[:, kc, a * 128:(a + 1) * 128], act[:, kc, :],
                                       start=(kc == 0), stop=(kc == 7))
                return ins
            S.op("pe", f, reads=[slot_reg, act_r], writes=[br])
            return bv, br

        def mm_tok(slot, slot_reg, act, act_r, bank=None):
            bk, br = next_bank() if bank is None else (banks[bank], bank_r[bank])
            bv = bk[:].rearrange("p (j c) -> p j c", j=2)

            def f(e):
                ins = None
                for j in range(2):
                    for kc in range(8):
                        ins = e.matmul(bv[:, j, :], act[:, kc, j * 128:(j + 1) * 128], slot[:, kc, :],
                                       start=(kc == 0), stop=(kc == 7))
                return ins
            S.op("pe", f, reads=[slot_reg, act_r], writes=[br])
            return bv, br

        def layer_norm(b, grow, brow, tag):
            Xb = X[b]
            st_r = reg("lnst")
            for j in range(2):
                for hf in range(2):
                    br_ = reg(f"bst{j}{hf}")
                    S.op("dve", lambda e, j=j, hf=hf: e.bn_stats(out=bst[:, j, hf, :],
                                                                 in_=Xb[:, j, hf * 512:(hf + 1) * 512]),
                         reads=[Xr[b][j]], writes=[br_])
                    S.op("dve", lambda e, j=j, hf=hf: e.bn_aggr(out=mv[:, j, hf, :], in_=bst[:, j, hf, :]),
                         reads=[br_], writes=[reg("mv")])
            mA, mB = mv[:, :, 0, 0], mv[:, :, 1, 0]
            vA, vB = mv[:, :, 0, 1], mv[:, :, 1, 1]
            rmv = reg("mv")
            S.op("dve", lambda e: e.tensor_tensor(out=lnst[:, 0:2], in0=mA, in1=mB, op=ALU.add),
                 reads=[rmv], writes=[reg("ln_sm")])
            S.op("dve", lambda e: e.tensor_tensor(out=lnst[:, 2:4], in0=mA, in1=mB, op=ALU.subtract),
                 reads=[rmv], writes=[reg("ln_dm")])
            S.op("dve", lambda e: e.tensor_tensor(out=lnst[:, 4:6], in0=vA, in1=vB, op=ALU.add),
                 reads=[rmv], writes=[reg("ln_sv")])
            S.op("dve", lambda e: e.tensor_tensor(out=lnst[:, 6:8], in0=lnst[:, 2:4], in1=lnst[:, 2:4], op=ALU.mult),
                 reads=[reg("ln_dm")], writes=[reg("ln_d2")])
            S.op("dve", lambda e: e.scalar_tensor_tensor(out=lnst[:, 8:10], in0=lnst[:, 6:8], scalar=0.5,
                                                         in1=lnst[:, 4:6], op0=ALU.mult, op1=ALU.add),
                 reads=[reg("ln_d2"), reg("ln_sv")], writes=[st_r])
            S.op("act", lambda e: e.activation(out=lnst[:, 10:12], in_=lnst[:, 8:10], func=AF.Ln, bias=LN_EPS,
                                               scale=0.5),
                 reads=[st_r], writes=[st_r])
            S.op("act", lambda e: e.activation(out=lnst[:, 10:12], in_=lnst[:, 10:12], func=AF.Exp, scale=-0.5),
                 reads=[st_r], writes=[st_r])
            S.op("dve", lambda e: e.scalar_tensor_tensor(out=lnst[:, 12:14], in0=lnst[:, 0:2], scalar=-0.5,
                                                         in1=lnst[:, 10:12], op0=ALU.mult, op1=ALU.mult),
                 reads=[st_r, reg("ln_sm")], writes=[st_r])
            for j in range(2):
                xj = Xr[b][j]
                S.op("act", lambda e, j=j: e.activation(out=Xb[:, j, :], in_=Xb[:, j, :], func=AF.Identity,
                                                        scale=lnst[:, 10 + j:11 + j], bias=lnst[:, 12 + j:13 + j]),
                     reads=[xj, st_r], writes=[xj])
                S.op("dve", lambda e, j=j: e.tensor_tensor(out=Xb[:, j, :], in0=Xb[:, j, :], in1=lnp[:, grow, :],
                                                           op=ALU.mult),
                     reads=[xj, reg(f"lnp{grow}")], writes=[xj])
                S.op("dve", lambda e, j=j: e.tensor_tensor(out=Xb[:, j, :], in0=Xb[:, j, :], in1=lnp[:, brow, :],
                                                           op=ALU.add),
                     reads=[xj, reg(f"lnp{brow}")], writes=[xj])

        F1f = [F1[g][:].rearrange("p a t -> p (a t)") for g in range(2)]
        qsf = [qs[g][:].rearrange("p a t -> p (a t)") for g in range(2)]
        F2f = F2[:].rearrange("p a t -> p (a t)")
        F3f = F3[:].rearrange("p a t -> p (a t)")
        Ef = E[:].rearrange("p a t -> p (a t)")
        G3 = F3f.rearrange("p (c t) -> p c t", t=64)
        Gd3 = F2f.rearrange("p (c t) -> p c t", t=64)
        E3 = Ef.rearrange("p (c t) -> p c t", t=64)

        def stageA0(ti):
            b = ti % 2
            ring_state["lim"] = 7
            transposes(X[b], Xr[b], xT, reg("xT"), "act")

        def stageA(ti):
            b = ti % 2
            ring_state["lim"] = 7
            for g in range(2):
                rF1, rqs = reg(f"F1_{g}"), reg(f"qs_{g}")
                for p2 in range(2):
                    slot, sr = get_piece(ti, PB_WIN + 4 + 2 * g + p2)
                    bv, br = mm_feat(slot, sr, xT, reg("xT"))
                    S.op("act", lambda e, bv=bv, p2=p2, g=g: e.activation(out=F1[g][:, 2 * p2:2 * p2 + 2, :],
                                                                          in_=bv, func=AF.Sigmoid),
                         reads=[br], writes=[rF1])
                for p2 in range(2):
                    slot, sr = get_piece(ti, PB_WIN + 0 + 2 * g + p2)
                    bv, br = mm_feat(slot, sr, xT, reg("xT"))
                    S.op("act", lambda e, bv=bv, p2=p2, g=g: e.activation(out=qs[g][:, 2 * p2:2 * p2 + 2, :],
                                                                          in_=bv, func=AF.Silu),
                         reads=[br], writes=[rqs])
            for pv in range(4):
                slot, sr = get_piece(ti, PB_WIN + 8 + pv)
                bv, br = mm_tok(slot, sr, xT, reg("xT"))
                S.op("dve", lambda e, bv=bv, pv=pv: e.tensor_copy(out=v_tok[:, :, pv * 256:(pv + 1) * 256], in_=bv),
                     reads=[br], writes=[reg("v_tok")])
            for po in range(4):
                slot, sr = get_piece(ti, PB_WIN + 12 + po)
                bv, br = mm_feat(slot, sr, xT, reg("xT"))
                S.op("act", lambda e, bv=bv, po=po: e.activation(out=og_s[:, 2 * po:2 * po + 2, :], in_=bv,
                                                                 func=AF.Sigmoid),
                     reads=[br], writes=[reg("og_s")])
            steps = []
            rF2, rF3, rE = reg("F2"), reg("F3"), reg("E")
            for g in range(2):
                h0 = 4 * g
                rF1, rqs = reg(f"F1_{g}"), reg(f"qs_{g}")
                for hh in range(4):
                    h = h0 + hh
                    steps.append(lambda hh=hh, h=h, g=g, rF1=rF1: S.op(
                        "dve", lambda e: e.tensor_scalar(out=F1[g][:, hh, :], in0=F1[g][:, hh, :],
                                                         scalar1=lbt[:, 8 + h:9 + h], scalar2=lbt[:, h:h + 1],
                                                         op0=ALU.mult, op1=ALU.add),
                        reads=[rF1, reg("lb"), reg("oml")], writes=[rF1]))
                steps.append(lambda g=g, rF1=rF1: S.op(
                    "act", lambda e: e.activation(out=F2f, in_=F1f[g], func=AF.Ln), reads=[rF1], writes=[rF2]))
                steps.append(lambda: S.op(
                    "dve", lambda e: e.tensor_tensor_scan(out=F3f, data0=reset_b[:], data1=F2f, initial=0.0,
                                                          op0=ALU.mult, op1=ALU.add),
                    reads=[rF2, reg("reset_b")], writes=[rF3]))
                steps.append(lambda g=g, rF1=rF1: S.op(
                    "dve", lambda e: e.tensor_scalar(out=F1f[g], in0=F1f[g], scalar1=-1.0, scalar2=1.0,
                                                     op0=ALU.mult, op1=ALU.add),
                    reads=[rF1], writes=[rF1]))
                steps.append(lambda: S.op(
                    "dve", lambda e: e.tensor_tensor(out=Gd3, in0=G3[:, :, 63:64].to_broadcast([128, 16, 64]),
                                                     in1=G3, op=ALU.subtract),
                    reads=[rF3], writes=[rF2]))
                steps.append(lambda: S.op(
                    "act", lambda e: e.activation(out=Ef, in_=F3f, func=AF.Exp), reads=[rF3], writes=[rE]))
                steps.append(lambda g=g: S.op(
                    "dve", lambda e: e.tensor_copy(out=dec[:, g, :], in_=E3[:, :, 63]),
                    reads=[rE], writes=[reg(f"dec{g}")]))
                steps.append(lambda g=g, h0=h0, rqs=rqs: S.op(
                    "dve", lambda e: e.scalar_tensor_tensor(
                        out=q_dec[:, h0:h0 + 4, :].rearrange("p a t -> p (a t)"), in0=qsf[g], scalar=QSCALE, in1=Ef,
                        op0=ALU.mult, op1=ALU.mult),
                    reads=[rqs, rE], writes=[reg(f"q_dec{g}")]))
                steps.append(lambda: S.op(
                    "act", lambda e: e.activation(out=Ef, in_=F3f, func=AF.Exp, scale=-1.0),
                    reads=[rF3], writes=[rE]))
                steps.append(lambda g=g, h0=h0, rF1=rF1: S.op(
                    "dve", lambda e: e.tensor_tensor(out=k_inv[:, h0:h0 + 4, :].rearrange("p a t -> p (a t)"),
                                                     in0=F1f[g], in1=Ef, op=ALU.mult),
                    reads=[rF1, rE], writes=[reg(f"k_inv{g}")]))
                steps.append(lambda: S.op(
                    "act", lambda e: e.activation(out=F2f, in_=F2f, func=AF.Exp), reads=[rF2], writes=[rF2]))
                steps.append(lambda g=g, h0=h0, rF1=rF1: S.op(
                    "dve", lambda e: e.tensor_tensor(out=k_end[:, h0:h0 + 4, :].rearrange("p a t -> p (a t)"),
                                                      in0=F1f[g], in1=F2f, op=ALU.mult),
                    reads=[rF1, rF2], writes=[reg(f"k_end{g}")]))
            return steps

        deferred = []

        def stageB(ti):
            b = ti % 2
            first = (ti % TILES_PER_SEQ == 0)
            Xb, xr = X[b], Xr[b]
            if first:
                S.op("dve", lambda e: e.memset(S32[:], 0.0), writes=[reg(f"S32_{h}") for h in range(8)])
                S.op("pool", lambda e: e.memset(S16[:], 0.0), writes=[reg(f"S16g{g}") for g in range(2)])
            if ti == 0:
                dump("xT", xT[:], [reg("xT")])
                dump("q_dec", q_dec[:], [reg("q_dec0"), reg("q_dec1")])
                dump("k_inv", k_inv[:], [reg("k_inv0"), reg("k_inv1")])
                dump("k_end", k_end[:], [reg("k_end0"), reg("k_end1")])
                dump("v_tok", v_tok[:], [reg("v_tok")])

            ring_state["lim"] = 3
            ring_state["k"] = 0
            bT = bankT[:].rearrange("p (j a d) -> p j a d", j=2, a=4)
            for g in range(2):
                def trk(e, g=g):
                    ins = None
                    for j in range(2):
                        for hh in range(4):
                            ins = e.transpose(out=bT[:, j, hh, :], in_=k_end[:, 4 * g + hh, j * 128:(j + 1) * 128],
                                              identity=ident_b[:])
                    return ins
                S.op("pe", trk, reads=[reg(f"k_end{g}"), reg("ident_b")], writes=[bankT_r])
                S.op("act", lambda e, g=g: e.activation(out=k_endT[:, g].rearrange("p j a d -> p (j a d)"),
                                                        in_=bankT[:], func=AF.Copy),
                     reads=[bankT_r], writes=[reg(f"k_endT{g}")])
            ob = [banks[3], banks[4], banks[5], banks[6]]
            obr = [bank_r[3], bank_r[4], bank_r[5], bank_r[6]]
            obv = [bb[:].rearrange("p (a t) -> p a t", a=2) for bb in ob]
            sbk, sbr = banks[2], bank_r[2]
            sv = sbk[:].rearrange("p (a t) -> p a t", a=4)
            for j in range(2):
                for g in range(2):
                    def scf(e, j=j, g=g):
                        ins = None
                        for hh in range(4):
                            h = 4 * g + hh
                            ins = e.matmul(sv[:, hh, :], k_inv[:, h, j * 128:(j + 1) * 128],
                                           q_dec[:, h, j * 128:(j + 1) * 128], start=True, stop=True)
                        return ins
                    S.op("pe", scf, reads=[reg(f"k_inv{g}"), reg(f"q_dec{g}")], writes=[sbr])
                    S.op("dve", lambda e, j=j, g=g: e.tensor_tensor(out=sc[2 * j + g][:], in0=sv, in1=mask_b[:],
                                                                   op=ALU.mult),
                         reads=[sbr, reg("mask_b")], writes=[reg(f"sc{2 * j + g}")])
                for cc in range(2):
                    c = 2 * j + cc
                    p0 = 64 * cc
                    for g in range(2):
                        stv = banks[g][:].rearrange("p (a t) -> p a t", a=4)

                        def stf(e, j=j, p0=p0, stv=stv, g=g):
                            ins = None
                            for hh in range(4):
                                h = 4 * g + hh
                                ins = e.matmul(stv[:, hh, :], k_endT[p0:p0 + 64, g, j, hh, :],
                                               v_tok[p0:p0 + 64, j, h * 128:(h + 1) * 128], start=True, stop=True)
                            return ins
                        S.op("pe", stf, reads=[reg(f"k_endT{g}"), reg("v_tok")], writes=[bank_r[g]])
                    for g in range(2):
                        scs = sc[2 * j + g]

                        def of(e, g=g, j=j, p0=p0, c=c, scs=scs):
                            ins = None
                            for hh in range(4):
                                h = 4 * g + hh
                                dst = obv[h // 2][:, h % 2, c * 64:(c + 1) * 64]
                                e.matmul(dst, v_tok[p0:p0 + 64, j, h * 128:(h + 1) * 128],
                                         scs[p0:p0 + 64, hh, p0:p0 + 64], start=True, stop=False)
                                ins = e.matmul(dst, S16[:, h, :], q_dec[:, h, c * 64:(c + 1) * 64],
                                               start=False, stop=True)
                            return ins
                        S.op("pe", of, reads=[reg("v_tok"), reg(f"sc{2 * j + g}"), reg(f"S16g{g}"), reg(f"q_dec{g}")],
                             writes=[obr[2 * g], obr[2 * g + 1]])
                    for g in range(2):
                        stv = banks[g][:].rearrange("p (a t) -> p a t", a=4)
                        for hh in range(4):
                            h = 4 * g + hh
                            dcol = dec[:, g, hh * 4 + c:hh * 4 + c + 1]
                            S.op("dve", lambda e, h=h, hh=hh, dcol=dcol, stv=stv: e.scalar_tensor_tensor(
                                out=S32[:, h, :], in0=S32[:, h, :], scalar=dcol, in1=stv[:, hh, :],
                                op0=ALU.mult, op1=ALU.add),
                                 reads=[reg(f"S32_{h}"), reg(f"dec{g}"), bank_r[g]], writes=[reg(f"S32_{h}")])
                        S.op("act", lambda e, g=g: e.activation(out=S16[:, 4 * g:4 * g + 4, :],
                                                                in_=S32[:, 4 * g:4 * g + 4, :], func=AF.Copy),
                             reads=[reg(f"S32_{4 * g + hh}") for hh in range(4)], writes=[reg(f"S16g{g}")])
            qbs = {}

            def n1(r4):
                r2 = r4 % 2
                bkv, bkr = obv[r4], obr[r4]
                osq_, o32_ = osq[r2], o32[r2]
                S.op("act", lambda e: e.activation(out=osq_[:], in_=bkv, func=AF.Square),
                     reads=[bkr], writes=[reg(f"osq{r2}")])
                S.op("act", lambda e: e.activation(out=o32_[:], in_=bkv, func=AF.Copy),
                     reads=[bkr], writes=[reg(f"o32{r2}")])
                qb, qbr = banks[r4 % 2], bank_r[r4 % 2]
                qbs[r4] = (qb, qbr)
                S.op("pe", lambda e: e.matmul(qb[:], ones_b[:], osq_[:].rearrange("p a t -> p (a t)"),
                                              start=True, stop=True),
                     reads=[reg(f"osq{r2}"), reg("ones_b")], writes=[qbr])

            def n2(r4):
                r2 = r4 % 2
                o32_, lt_ = o32[r2], lt[r2]
                qb, qbr = qbs[r4]
                ltf = lt_[:].rearrange("p a t -> p (a t)")
                S.op("act", lambda e: e.activation(out=ltf, in_=qb[:], func=AF.Ln, bias=RMS_EPS),
                     reads=[qbr], writes=[reg(f"lt{r2}")])
                S.op("act", lambda e: e.activation(out=ltf, in_=ltf, func=AF.Exp, scale=-0.5),
                     reads=[reg(f"lt{r2}")], writes=[reg(f"lt{r2}")])
                S.op("dve", lambda e: e.tensor_tensor(out=o32_[:], in0=o32_[:], in1=lt_[:], op=ALU.mult),
                     reads=[reg(f"o32{r2}"), reg(f"lt{r2}")], writes=[reg(f"o32{r2}")])
                for a in range(2):
                    h = 2 * r4 + a
                    S.op("dve", lambda e, h=h, a=a: e.scalar_tensor_tensor(
                        out=o_gated[:, h, :], in0=o32_[:, a, :], scalar=gcol[:, h:h + 1], in1=og_s[:, h, :],
                        op0=ALU.mult, op1=ALU.mult),
                         reads=[reg(f"o32{r2}"), reg("cols"), reg("og_s")], writes=[reg("o_gated")])

            slots3 = [(2 * ti) % 3, (2 * ti + 1) % 3, (2 * ti + 2) % 3]
            pvr = [reg(f"pool_v{s_}") for s_ in range(3)]

            def pool_branch(g, bank_a=None, bank_b=None):
                slot, sr = get_piece(ti, PB_WIN + 16 + g)
                bv, br = mm_tok(slot, sr, xT, reg("xT"), bank=bank_a)
                for j in range(2):
                    s_ = slots3[1 + j]
                    S.op("dve", lambda e, j=j, s_=s_: e.tensor_copy(
                        out=pool_v[:, s_, g * 256:(g + 1) * 256], in_=bv[:, j, :]),
                         reads=[br], writes=[pvr[s_]])
                bk, bkr = next_bank() if bank_b is None else (banks[bank_b], bank_r[bank_b])
                bkv = bk[:].rearrange("p (a t) -> p a t", a=2)

                def pf(e):
                    ins = None
                    for cc in range(2):
                        k8 = 2 * g + cc
                        for j in range(2):
                            start_tile = first and j == 0
                            ptm = pt_b[:, (8 + g) if start_tile else g, :]
                            dst = bkv[:, cc, j * 128:(j + 1) * 128]
                            ins = e.matmul(dst, pool_v[:, slots3[1 + j], k8 * 128:(k8 + 1) * 128], ptm,
                                           start=True, stop=start_tile)
                            if not start_tile:
                                ins = e.matmul(dst, pool_v[:, slots3[j], k8 * 128:(k8 + 1) * 128],
                                               pt_b[:, 4 + g, :], start=False, stop=True)
                    return ins
                S.op("pe", pf, reads=[pvr[0], pvr[1], pvr[2], reg("pt_b")], writes=[bkr])
                S.op("act", lambda e: e.activation(out=pooledT[:, 2 * g:2 * g + 2, :], in_=bkv, func=AF.Copy),
                     reads=[bkr], writes=[reg("pooledT")])

            n1(0)
            n1(1)
            pool_branch(0, bank_a=2, bank_b=3)
            n2(0)
            n2(1)
            n1(2)
            n1(3)
            n2(2)
            n2(3)
            ring_state["lim"] = 7
            pool_branch(1)
            if ti == 0:
                dump("o_gated", o_gated[:], [reg("o_gated")])
            pool_branch(2)
            pool_branch(3)
            if ti == 0:
                dump("pooledT", pooledT[:], [reg("pooledT")])
            while deferred:
                deferred.pop(0)()

            for np_ in range(4):
                r2 = np_ % 2
                slot, sr = get_piece(ti, PB_WIN + 20 + np_)
                bv, br = mm_feat(slot, sr, xT, reg("xT"))
                S.op("act", lambda e, bv=bv, r2=r2: e.activation(out=gA[r2][:], in_=bv, func=AF.Sigmoid),
                     reads=[br], writes=[reg(f"gA{r2}")])
                slot, sr = get_piece(ti, PB_WIN + 24 + np_)
                bv, br = mm_feat(slot, sr, xT, reg("xT"))
                S.op("act", lambda e, bv=bv, r2=r2: e.activation(out=gB[r2][:], in_=bv, func=AF.Sigmoid),
                     reads=[br], writes=[reg(f"gB{r2}")])
                bk, bkr = next_bank()
                bkv = bk[:].rearrange("p (a t) -> p a t", a=2)

                def bf(e, np_=np_, bkv=bkv):
                    ins = None
                    for nn in range(2):
                        for cc in range(2):
                            ins = e.matmul(bkv[:, nn, :], wpool_sb[:, 2 * np_ + cc, nn * 128:(nn + 1) * 128],
                                           pooledT[:, 2 * np_ + cc, :], start=(cc == 0), stop=(cc == 1))
                    return ins
                S.op("pe", bf, reads=[reg("wpool_sb"), reg("pooledT")], writes=[bkr])
                for nn in range(2):
                    S.op("dve", lambda e, nn=nn, np_=np_, bkv=bkv, r2=r2: e.scalar_tensor_tensor(
                        out=t1[r2][:, nn, :], in0=bkv[:, nn, :], scalar=pscol[:, 2 * np_ + nn:2 * np_ + nn + 1],
                        in1=gB[r2][:, nn, :], op0=ALU.mult, op1=ALU.mult),
                         reads=[bkr, reg("cols"), reg(f"gB{r2}")], writes=[reg(f"t1{r2}")])
                slot, sr = get_piece(ti, PB_WA + np_)
                bv, br = mm_feat(slot, sr, o_gated, reg("o_gated"))
                S.op("dve", lambda e, bv=bv, r2=r2: e.tensor_tensor(out=t2[r2][:], in0=bv, in1=gA[r2][:], op=ALU.mult),
                     reads=[br, reg(f"gA{r2}")], writes=[reg(f"t2{r2}")])
                S.op("dve", lambda e, np_=np_, r2=r2: e.tensor_tensor(out=merged[:, 2 * np_:2 * np_ + 2, :],
                                                                      in0=t2[r2][:], in1=t1[r2][:], op=ALU.add),
                     reads=[reg(f"t2{r2}"), reg(f"t1{r2}")], writes=[reg("merged")])
            if ti == 0:
                dump("merged", merged[:], [reg("merged")])

            if ti + 1 < ntiles:
                flush_pending(force=True)
                load_x(ti + 1)

            for mq in range(4):
                slot, sr = get_piece(ti, PB_WOUT + mq)
                bv, br = mm_tok(slot, sr, merged, reg("merged"))
                S.op("dve", lambda e, bv=bv, mq=mq: e.scalar_tensor_tensor(
                    out=Xb[:, :, mq * 256:(mq + 1) * 256], in0=Xb[:, :, mq * 256:(mq + 1) * 256], scalar=ALPHA,
                    in1=bv, op0=ALU.mult, op1=ALU.add),
                     reads=[br] + xr, writes=xr)
            if ti == 0:
                dump("y1", Xb[:], xr)
            if ti + 1 < ntiles:
                stageA0(ti + 1)
            layer_norm(b, 0, 1, "ln1")
            if ti == 0:
                dump("x1", Xb[:], xr)

        def stageC(ti, steps):
            b = ti % 2
            Xb, xr = X[b], Xr[b]
            ring_state["lim"] = 7
            transposes(Xb, xr, x1T, reg("x1T"), "dve")
            for u in range(16):
                slot, sr = get_piece(ti, PB_WUP + u)
                bv, br = mm_feat(slot, sr, x1T, reg("x1T"))
                r2 = u % 2
                S.op("act", lambda e, bv=bv, r2=r2: e.activation(out=rtmp[r2][:], in_=bv, func=AF.Relu),
                     reads=[br], writes=[reg(f"rtmp{r2}")])
                if u % 2 == 0:
                    S.op("dve", lambda e, u=u, r2=r2: e.tensor_tensor(out=hT[:, 2 * u:2 * u + 2, :], in0=rtmp[r2][:],
                                                                     in1=rtmp[r2][:], op=ALU.mult),
                         reads=[reg(f"rtmp{r2}")], writes=[reg(f"hT{u // 4}")])
                else:
                    S.op("act", lambda e, u=u, r2=r2: e.activation(out=hT[:, 2 * u:2 * u + 2, :], in_=rtmp[r2][:],
                                                                  func=AF.Square),
                         reads=[reg(f"rtmp{r2}")], writes=[reg(f"hT{u // 4}")])
                if steps:
                    steps.pop(0)()
            for mq in range(4):
                bj = [next_bank() for _ in range(2)]
                for dp in range(4):
                    slot, sr = get_piece(ti, PB_WDOWN + mq * 4 + dp)

                    def df(e, dp=dp, slot=slot, bj=bj):
                        ins = None
                        for j in range(2):
                            for fk in range(8):
                                ins = e.matmul(bj[j][0][:, 0:256], hT[:, dp * 8 + fk, j * 128:(j + 1) * 128],
                                               slot[:, fk, :],
                                               start=(dp == 0 and fk == 0), stop=(dp == 3 and fk == 7))
                        return ins
                    S.op("pe", df, reads=[sr, reg(f"hT{dp}")], writes=[bj[0][1], bj[1][1]])
                    if steps:
                        steps.pop(0)()
                for j in range(2):
                    S.op("dve", lambda e, bj=bj, mq=mq, j=j: e.scalar_tensor_tensor(
                        out=Xb[:, j, mq * 256:(mq + 1) * 256], in0=Xb[:, j, mq * 256:(mq + 1) * 256], scalar=ALPHA,
                        in1=bj[j][0][:, 0:256], op0=ALU.mult, op1=ALU.add),
                         reads=[bj[j][1], xr[j]], writes=[xr[j]])
            while steps:
                steps.pop(0)()
            if ti == 0:
                dump("hT", hT[:], [reg(f"hT{k}") for k in range(4)])
                dump("y2", Xb[:], xr)

            def ln2_store(ti=ti, b=b):
                layer_norm(b, 2, 3, "ln2")
                pending.append([3, lambda: store_out(ti)])
            deferred.append(ln2_store)

        load_x(0)
        stageA0(0)
        steps0 = stageA(0)
        while steps0:
            steps0.pop(0)()
        for ti in range(ntiles):
            stageB(ti)
            nsteps = stageA(ti + 1) if ti + 1 < ntiles else []
            stageC(ti, nsteps)
        while deferred:
            deferred.pop(0)()
        flush_pending(force=True)
        S.op("sp", lambda e: None, reads=[reg("outdram")])
        S.emit()
    return nc


def _pieces(w, ncolpieces):
    k = w.shape[0] // 128
    return np.ascontiguousarray(w.reshape(k, 128, ncolpieces, 256).transpose(2, 1, 0, 3)).reshape(ncolpieces, 128, k * 256)


def _constants():
    ident = np.eye(128, dtype=np.float32)
    s = np.arange(128)[:, None]
    t = np.arange(128)[None, :]
    m = ((t >= s) & ((t // 64) == (s // 64))).astype(np.float32)
    mask = np.tile(m, (1, 4))
    reset = np.ones((128, 1024), np.float32)
    reset[:, ::64] = 0.0
    pt = np.zeros((128, 12, 128), np.float32)
    for g, w in enumerate(POOL_W):
        d = t - s
        band = ((d >= 0) & (d < w)).astype(np.float32)
        pt[:, g, :] = band / w - (d == 0)
        dprev = t + 128 - s
        pt[:, 4 + g, :] = (dprev < w).astype(np.float32) / w
        cnt = np.minimum(t + 1, w).astype(np.float32)
        pt[:, 8 + g, :] = band / cnt - (d == 0)
    return ident, mask, reset, pt.reshape(128, 12 * 128)


def _host_inputs(x, w_in, lb_logits, hgrn_norm_g, w_a, w_pool, pool_scale, w_out,
                 ln1_g, ln1_b, w_up, w_down, ln2_g, ln2_b):
    f = np.float32
    pieces = [
        _pieces(np.asarray(w_in[0], f), 28),
        _pieces(np.asarray(w_a[0], f), 4),
        _pieces(np.asarray(w_out[0], f), 4),
        _pieces(np.asarray(w_up[0], f), 16),
    ]
    wd = np.asarray(w_down[0], f).reshape(4, 8, 128, 4, 256)
    pieces.append(np.ascontiguousarray(wd.transpose(3, 0, 2, 1, 4)).reshape(16, 128, 2048))
    wp = np.concatenate(pieces, axis=0)
    assert wp.shape == (NPIECE, 128, 2048)
    wpool = np.ascontiguousarray(np.asarray(w_pool[0], f).reshape(4, 2, 128, 256).transpose(2, 0, 1, 3)).reshape(128, 2048)

    def col(v):
        return np.asarray(v, f).reshape(8, 128).T

    cols = np.ascontiguousarray(np.concatenate(
        [col(lb_logits[0]), col(lb_logits[1]), col(hgrn_norm_g[0]), col(pool_scale[0])], axis=1))
    rows = np.ascontiguousarray(np.stack([np.asarray(v[0], f) for v in (ln1_g, ln1_b, ln2_g, ln2_b)]))
    ident, mask, reset, pt = _constants()
    shared = {"wp": wp, "wpool": wpool, "cols": cols, "rows": rows, "c_ident": ident, "c_mask": mask,
              "c_reset": reset, "c_pt": pt}
    xs = np.asarray(x, f).reshape(N_CORES, TOK_CORE, D)
    return [dict(shared, x=np.ascontiguousarray(xs[c])) for c in range(N_CORES)]


def kernel(x, w_in, lb_logits, hgrn_norm_g, w_a, w_pool, pool_scale, w_out,
           ln1_g, ln1_b, w_up, w_down, ln2_g, ln2_b):
    in_maps = _host_inputs(x, w_in, lb_logits, hgrn_norm_g, w_a, w_pool, pool_scale, w_out,
                           ln1_g, ln1_b, w_up, w_down, ln2_g, ln2_b)
    nc = build_nc()
    res = run_bass_kernel_spmd(nc, in_maps, core_ids=list(range(N_CORES)))
    out = np.stack([np.asarray(r["out"], np.float32) for r in res.results], axis=0)
    return out.reshape(16, SEQ, D)
```
